# Optimizing a Trainium2 kernel written in Bass

```python
import jax, jax.numpy as jnp
from jax import lax
import numpy as np

D_MODEL = 1024
BATCH = 16
SEQ = 2048
DEPTH = 1

HEAD_DIM = 64
MOBA_HEADS = 8
MOBA_BLOCK = 256
MOBA_TOPK = 3
MOBA_Q_CHUNK = 8
DSA_HEADS = 8
DSA_TOPK = 256
DSA_Q_CHUNK = 16
IDX_HEADS = 8
IDX_DIM = 64
MEM_LEN = 256
X_HEADS = 4
X_HEAD_DIM = 128
ROPE_THETA = 10000.0
RMS_EPS = 1e-6
N_BRANCH = 3
MOBA_W = MOBA_HEADS * HEAD_DIM
DSA_W = DSA_HEADS * HEAD_DIM
X_W = X_HEADS * X_HEAD_DIM
IN_SIZES = (MOBA_W, MOBA_W, MOBA_W, MOBA_W, DSA_W, DSA_W, DSA_W, DSA_W,
            IDX_HEADS * IDX_DIM, IDX_DIM, IDX_HEADS, X_W, X_W, N_BRANCH * D_MODEL)
IN_WIDTH = 4 * MOBA_W + 4 * DSA_W + IDX_HEADS * IDX_DIM + IDX_DIM + IDX_HEADS + 2 * X_W + N_BRANCH * D_MODEL

kernel_name = 'hybrid_moba_dsa_xattn_gated_block'

NEG_INF = -jnp.inf


def rmsnorm(x, g):
    xf = x.astype(jnp.float32)
    y = xf * lax.rsqrt(jnp.mean(xf * xf, axis=-1, keepdims=True) + RMS_EPS)
    return (y * g.astype(jnp.float32)).astype(x.dtype)


def rope(x, pos):
    d = x.shape[-1]
    half = d // 2
    inv = jnp.power(ROPE_THETA, -jnp.arange(half, dtype=jnp.float32) * 2.0 / d)
    ang = pos.astype(jnp.float32)[:, None] * inv[None, :]
    cos = jnp.cos(ang)[None, :, None, :]
    sin = jnp.sin(ang)[None, :, None, :]
    xf = x.astype(jnp.float32)
    x1, x2 = xf[..., :half], xf[..., half:]
    return jnp.concatenate([x1 * cos - x2 * sin, x2 * cos + x1 * sin], axis=-1).astype(x.dtype)


def moba_attention(q, k, v):
    B, S, H, D = q.shape
    nb = -(-S // MOBA_BLOCK)
    sp = nb * MOBA_BLOCK
    padw = ((0, 0), (0, sp - S), (0, 0), (0, 0))
    qp, kp, vp = jnp.pad(q, padw), jnp.pad(k, padw), jnp.pad(v, padw)
    scale = D ** -0.5
    kbh = kp.reshape(B, nb, MOBA_BLOCK, H, D).transpose(0, 3, 1, 2, 4)
    vbh = vp.reshape(B, nb, MOBA_BLOCK, H, D).transpose(0, 3, 1, 2, 4)
    kmean = jnp.mean(kbh.astype(jnp.float32), axis=3)
    gate = jnp.einsum('bshd,bhnd->bshn', qp.astype(jnp.float32), kmean)
    qblk = jnp.arange(sp) // MOBA_BLOCK
    past = jnp.arange(nb)[None, :] < qblk[:, None]
    gate = jnp.where(past[None, :, None, :], gate, NEG_INF)
    n_sel = max(min(MOBA_TOPK, nb - 1), 1)
    top_val, top_idx = lax.top_k(gate, n_sel)
    top_ok = jnp.isfinite(top_val)
    nc = sp // MOBA_Q_CHUNK
    b_ix = jnp.arange(B)[:, None, None, None]
    h_ix = jnp.arange(H)[None, None, :, None]

    def to_chunks(a):
        return a.reshape((B, nc, MOBA_Q_CHUNK) + a.shape[2:]).swapaxes(0, 1)

    def step(args):
        ci, q_c, idx_c, ok_c = args
        t0 = ci * MOBA_Q_CHUNK
        blk_start = (t0 // MOBA_BLOCK) * MOBA_BLOCK
        k_own = lax.dynamic_slice_in_dim(kp, blk_start, MOBA_BLOCK, axis=1)
        v_own = lax.dynamic_slice_in_dim(vp, blk_start, MOBA_BLOCK, axis=1)
        qpos = t0 + jnp.arange(MOBA_Q_CHUNK)
        kpos = blk_start + jnp.arange(MOBA_BLOCK)
        s_own = jnp.einsum('bqhd,bkhd->bhqk', q_c, k_own).astype(jnp.float32) * scale
        s_own = jnp.where((kpos[None, :] <= qpos[:, None])[None, None], s_own, NEG_INF)
        k_sel = kbh[b_ix, h_ix, idx_c]
        v_sel = vbh[b_ix, h_ix, idx_c]
        s_past = jnp.einsum('bqhd,bqhnkd->bhqnk', q_c, k_sel).astype(jnp.float32) * scale
        s_past = jnp.where(ok_c.transpose(0, 2, 1, 3)[..., None], s_past, NEG_INF)
        n_past = n_sel * MOBA_BLOCK
        s_all = jnp.concatenate([s_past.reshape(B, H, MOBA_Q_CHUNK, n_past), s_own], axis=-1)
        p = jax.nn.softmax(s_all, axis=-1).astype(q.dtype)
        p_past = p[..., :n_past].reshape(B, H, MOBA_Q_CHUNK, n_sel, MOBA_BLOCK)
        p_own = p[..., n_past:]
        return (jnp.einsum('bhqnk,bqhnkd->bqhd', p_past, v_sel)
                + jnp.einsum('bhqk,bkhd->bqhd', p_own, v_own))

    outs = lax.map(step, (jnp.arange(nc), to_chunks(qp), to_chunks(top_idx), to_chunks(top_ok)))
    return outs.swapaxes(0, 1).reshape(B, sp, H, D)[:, :S]


def dsa_attention(q, k, v, iq, ik, iw):
    B, S, H, D = q.shape
    k_top = min(DSA_TOPK, S // 4)
    nc = S // DSA_Q_CHUNK
    scale = D ** -0.5
    iscale = IDX_DIM ** -0.5
    wscale = IDX_HEADS ** -0.5
    b_ix = jnp.arange(B)[:, None, None]
    kpos = jnp.arange(S)

    def to_chunks(a):
        return a.reshape((B, nc, DSA_Q_CHUNK) + a.shape[2:]).swapaxes(0, 1)

    def step(args):
        ci, q_c, iq_c, iw_c = args
        qpos = ci * DSA_Q_CHUNK + jnp.arange(DSA_Q_CHUNK)
        logits = jnp.einsum('bqgd,bsd->bqgs', iq_c, ik).astype(jnp.float32) * iscale
        score = jnp.einsum('bqgs,bqg->bqs', jax.nn.relu(logits), iw_c.astype(jnp.float32) * wscale)
        score = jnp.where((kpos[None, :] <= qpos[:, None])[None], score, NEG_INF)
        top_val, top_idx = lax.top_k(score, k_top)
        ok = jnp.isfinite(top_val)
        k_sel = k[b_ix, top_idx]
        v_sel = v[b_ix, top_idx]
        s = jnp.einsum('bqhd,bqkhd->bhqk', q_c, k_sel).astype(jnp.float32) * scale
        s = jnp.where(ok[:, None], s, NEG_INF)
        p = jax.nn.softmax(s, axis=-1).astype(q.dtype)
        return jnp.einsum('bhqk,bqkhd->bqhd', p, v_sel)

    outs = lax.map(step, (jnp.arange(nc), to_chunks(q), to_chunks(iq), to_chunks(iw)))
    return outs.swapaxes(0, 1).reshape(B, S, H, D)


def cross_attention(q, mk, mv):
    scale = q.shape[-1] ** -0.5
    s = jnp.einsum('bshd,bmhd->bhsm', q, mk).astype(jnp.float32) * scale
    p = jax.nn.softmax(s, axis=-1).astype(q.dtype)
    return jnp.einsum('bhsm,bmhd->bshd', p, mv)


def hybrid_layer(x, mem, g_in, w_in, b_merge, g_mem, w_mem_kv, w_up_moba, w_up_dsa, w_up_cross, w_out):
    B, S, _ = x.shape
    pos = jnp.arange(S)
    h = rmsnorm(x, g_in)
    z = h @ w_in
    offs = np.cumsum(IN_SIZES)[:-1].tolist()
    (mq, mk, mv, mg, dq, dk, dv, dg, iq, ik, iw, xq, xg, glog) = jnp.split(z, offs, axis=-1)
    heads = lambda a, n, d: a.reshape(B, S, n, d)
    mq = rope(heads(mq, MOBA_HEADS, HEAD_DIM), pos)
    mk = rope(heads(mk, MOBA_HEADS, HEAD_DIM), pos)
    ya = moba_attention(mq, mk, heads(mv, MOBA_HEADS, HEAD_DIM)).reshape(B, S, MOBA_W)
    ya = (ya * jax.nn.silu(mg)) @ w_up_moba
    dq = rope(heads(dq, DSA_HEADS, HEAD_DIM), pos)
    dk = rope(heads(dk, DSA_HEADS, HEAD_DIM), pos)
    iq = rope(heads(iq, IDX_HEADS, IDX_DIM), pos)
    ik = rope(ik[:, :, None, :], pos)[:, :, 0, :]
    yb = dsa_attention(dq, dk, heads(dv, DSA_HEADS, HEAD_DIM), iq, ik, iw).reshape(B, S, DSA_W)
    yb = (yb * jax.nn.silu(dg)) @ w_up_dsa
    M = mem.shape[1]
    kv = rmsnorm(mem, g_mem) @ w_mem_kv
    xk = kv[..., :X_W].reshape(B, M, X_HEADS, X_HEAD_DIM)
    xv = kv[..., X_W:].reshape(B, M, X_HEADS, X_HEAD_DIM)
    yc = cross_attention(heads(xq, X_HEADS, X_HEAD_DIM), xk, xv).reshape(B, S, X_W)
    yc = (yc * jax.nn.silu(xg)) @ w_up_cross
    gates = jax.nn.sigmoid(glog + b_merge).reshape(B, S, N_BRANCH, D_MODEL)
    u = gates[:, :, 0] * ya + gates[:, :, 1] * yb + gates[:, :, 2] * yc
    return x + u @ w_out


def setup_inputs(seed: int = 0) -> dict:
    key = jax.random.key(seed)
    ks = jax.random.split(key, 12)
    f32 = jnp.float32
    nrm = lambda k, shape, s: jax.random.normal(k, shape, f32) * s
    return {
        'x': nrm(ks[0], (BATCH, SEQ, D_MODEL), 1.0),
        'mem': nrm(ks[1], (BATCH, MEM_LEN, D_MODEL), 1.0),
        'g_in': 1.0 + nrm(ks[2], (DEPTH, D_MODEL), 0.02),
        'w_in': nrm(ks[3], (DEPTH, D_MODEL, IN_WIDTH), D_MODEL ** -0.5),
        'b_merge': nrm(ks[4], (DEPTH, N_BRANCH * D_MODEL), 0.1),
        'g_mem': 1.0 + nrm(ks[5], (DEPTH, D_MODEL), 0.02),
        'w_mem_kv': nrm(ks[6], (DEPTH, D_MODEL, 2 * X_W), D_MODEL ** -0.5),
        'w_up_moba': nrm(ks[7], (DEPTH, MOBA_W, D_MODEL), MOBA_W ** -0.5),
        'w_up_dsa': nrm(ks[8], (DEPTH, DSA_W, D_MODEL), DSA_W ** -0.5),
        'w_up_cross': nrm(ks[9], (DEPTH, X_W, D_MODEL), X_W ** -0.5),
        'w_out': nrm(ks[10], (DEPTH, D_MODEL, D_MODEL), D_MODEL ** -0.5),
        'g_final': 1.0 + nrm(ks[11], (D_MODEL,), 0.02),
    }


def reference(x, mem, g_in, w_in, b_merge, g_mem, w_mem_kv, w_up_moba, w_up_dsa, w_up_cross, w_out, g_final):
    for l in range(DEPTH):
        x = hybrid_layer(x, mem, g_in[l], w_in[l], b_merge[l], g_mem[l], w_mem_kv[l],
                         w_up_moba[l], w_up_dsa[l], w_up_cross[l], w_out[l])
    return rmsnorm(x, g_final)
```

```python
from contextlib import ExitStack
import numpy as np
import concourse.bass as bass
import concourse.mybir as mybir
from concourse.bass_utils import run_bass_kernel_spmd

F32 = mybir.dt.float32
BF16 = mybir.dt.bfloat16
U32 = mybir.dt.uint32
ALU = mybir.AluOpType
AF = mybir.ActivationFunctionType
AX = mybir.AxisListType

NCORES = 8
NB = 2
SEQ = 2048
DM = 1024
MEM = 256
INW = 8776
NEG = -30000.0
NEGF = -1.0e30
NITER = 16
OFF = dict(mq=0, mk=512, mv=1024, mg=1536, dq=2048, dk=2560, dv=3072, dg=3584,
           iq=4096, ik=4608, iw=4672, xq=4680, xg=5192, gl=5704)
C_ID, C_R, C_CBT, C_CBQ, C_PW, C_END = 0, 128, 256, 768, 1280, 1280 + NITER


class TT:
    def __init__(self, h, name):
        self.h = h
        self.name = name
        self.w = None
        self.r = {}
        self.dsem = None
        self.dcnt = 0
        self.osem = None
        self.ocnt = 0
        self.psum = False

    def view(self, tag):
        return TT(self.h, self.name + "_" + str(tag))


class Sync:
    def __init__(self, nc):
        self.nc = nc
        self.eng = {"pe": nc.tensor, "act": nc.scalar, "dve": nc.vector, "pool": nc.gpsimd, "sp": nc.sync}
        self.sem = {k: nc.alloc_semaphore("sem_" + k) for k in ("pe", "act", "dve", "pool")}
        self.cnt = {k: 0 for k in self.sem}
        self.seen = {k: {} for k in self.eng}
        self.dsems = []
        self.osems = []
        self.nsb = 0
        self.outs = []
        self.dead = False

    def tile(self, stack, name, shape, dtype, psum=False):
        self.nsb += 1
        nm = "%s_%d" % (name, self.nsb)
        if psum:
            h = stack.enter_context(self.nc.psum_tensor(nm, shape, dtype))
        else:
            h = stack.enter_context(self.nc.sbuf_tensor(nm, shape, dtype))
        t = TT(h, nm)
        t.psum = psum
        return t

    def _wait(self, E, deps):
        best = {}
        for d in deps:
            if d is None:
                continue
            key, h, v, src = d
            if src == E and E == "pe":
                continue
            if self.seen[E].get(key, 0) >= v:
                continue
            if key not in best or best[key][1] < v:
                best[key] = (h, v)
        for key, (h, v) in best.items():
            self.eng[E].wait_ge(h, v)
            self.seen[E][key] = v

    def _deps(self, r, w):
        deps = []
        for t in r:
            deps.append(t.w)
        for t in w:
            deps.append(t.w)
            deps.extend(t.r.values())
        return deps

    def _record(self, me, r, w):
        for t in w:
            t.w = me
            t.r = {}
        for t in r:
            if t in w:
                continue
            old = t.r.get(me[0])
            if old is None or old[2] < me[2]:
                t.r[me[0]] = me

    def op(self, E, fn, r=(), w=()):
        if self.dead:
            return None
        w = list(w) + [t for t in r if t.psum and t not in w]
        self._wait(E, self._deps(r, w))
        inst = fn(self.eng[E])
        self.cnt[E] += 1
        inst.then_inc(self.sem[E], 1)
        me = (E, self.sem[E], self.cnt[E], E)
        self._record(me, r, w)
        return inst

    def dma(self, E, out, in_, t, load=True, **kw):
        if self.dead:
            return None
        if load:
            self._wait(E, self._deps((), (t,)))
        else:
            self._wait(E, self._deps((t,), ()))
        if load:
            if t.dsem is None:
                t.dsem = self.nc.alloc_semaphore("dsem_%d" % len(self.dsems))
                self.dsems.append(t)
            inst = self.eng[E].dma_start(out=out, in_=in_, **kw)
            t.dcnt += 16
            inst.then_inc(t.dsem, 16)
            me = ("d_" + t.name, t.dsem, t.dcnt, None)
            self._record(me, (), (t,))
        else:
            if t.osem is None:
                t.osem = self.nc.alloc_semaphore("osem_%d" % len(self.osems))
                self.osems.append(t)
            inst = self.eng[E].dma_start(out=out, in_=in_, **kw)
            t.ocnt += 16
            inst.then_inc(t.osem, 16)
            me = ("o_" + t.name, t.osem, t.ocnt, None)
            self._record(me, (t,), ())
            self.outs.append(me)
        return inst

    def barrier(self):
        if self.dead:
            return
        deps = [(k, self.sem[k], self.cnt[k], k) for k in self.sem if self.cnt[k] > 0]
        for t in self.dsems:
            if t.dcnt > 0:
                deps.append(("d_" + t.name, t.dsem, t.dcnt, None))
        for t in self.osems:
            if t.ocnt > 0:
                deps.append(("o_" + t.name, t.osem, t.ocnt, None))
        for E in self.eng:
            self._wait(E, deps)

    def finish(self):
        self._wait("sp", self.outs)
        self.barrier()


class StopBuild(Exception):
    pass


def build(dbg=None, nb_run=NB, stop=None):
    nc = bass.Bass("TRN2", target_bir_lowering=False)
    S = Sync(nc)
    x = nc.dram_tensor("x", [NB, SEQ, DM], F32, kind="ExternalInput").ap()
    mem = nc.dram_tensor("mem", [NB, MEM, DM], F32, kind="ExternalInput").ap()
    w_in = nc.dram_tensor("w_in", [DM, INW], F32, kind="ExternalInput").ap()
    w_kv = nc.dram_tensor("w_kv", [DM, 1024], F32, kind="ExternalInput").ap()
    w_up = nc.dram_tensor("w_up", [3, 512, DM], F32, kind="ExternalInput").ap()
    w_out = nc.dram_tensor("w_out", [DM, DM], F32, kind="ExternalInput").ap()
    vecs = nc.dram_tensor("vecs", [128, 40], F32, kind="ExternalInput").ap()
    gfin = nc.dram_tensor("gfin", [128, DM], F32, kind="ExternalInput").ap()
    ropet = nc.dram_tensor("ropet", [128, 2 * SEQ], F32, kind="ExternalInput").ap()
    consts = nc.dram_tensor("consts", [128, C_END], F32, kind="ExternalInput").ap()
    y = nc.dram_tensor("y", [NB, SEQ, DM], F32, kind="ExternalOutput").ap()
    dbg_out = {}
    if dbg:
        for k, shp in dbg.items():
            dbg_out[k] = nc.dram_tensor("dbg_" + k, list(shp), F32, kind="ExternalOutput").ap()

    op = S.op
    root = ExitStack()
    with root:
        pA = S.tile(root, "pA", [128, 512], F32, psum=True)
        pB = S.tile(root, "pB", [128, 512], F32, psum=True)
        pR = S.tile(root, "pR", [128, 512], F32, psum=True)
        pS = [S.tile(root, "pS%d" % i, [128, 512], F32, psum=True) for i in range(2)]
        pV = [S.tile(root, "pV%d" % i, [128, 512], F32, psum=True) for i in range(2)]
        pT = S.tile(root, "pT", [128, 8, 128], BF16, psum=True)
        pAB = [pA, pB]
        cosT = S.tile(root, "cosT", [128, SEQ], F32)
        sinT = S.tile(root, "sinT", [128, SEQ], F32)
        cst = S.tile(root, "cst", [128, C_END - C_CBQ], F32)
        identb = S.tile(root, "identb", [128, 128], BF16)
        Rb = S.tile(root, "Rb", [128, 128], BF16)
        cbTb = S.tile(root, "cbTb", [128, 2, 256], BF16)
        vec = S.tile(root, "vec", [128, 40], F32)
        epst = S.tile(root, "epst", [128, 1], F32)
        wst = [S.tile(root, "wst%d" % i, [128, 512], F32) for i in range(2)]
        hT = S.tile(root, "hT", [128, 8, SEQ], BF16)
        smallf = S.tile(root, "smallf", [128, 64], F32)
        ss_v = [smallf.view("ss%d" % i) for i in range(4)]
        rs_v = [smallf.view("rs%d" % i) for i in range(4)]

        S.dma("sp", cosT.h[:, :], ropet[:, 0:SEQ], cosT)
        S.dma("sp", sinT.h[:, :], ropet[:, SEQ:2 * SEQ], sinT)
        S.dma("sp", cst.h[:, :], consts[:, C_CBQ:C_END], cst)
        S.dma("sp", vec.h[:, :], vecs[:, :], vec)
        with ExitStack() as st_s:
            cst0 = S.tile(st_s, "cst0", [128, C_CBQ], F32)
            S.dma("sp", cst0.h[:, :], consts[:, 0:C_CBQ], cst0)
            op("dve", lambda e: e.tensor_copy(identb.h[:, :], cst0.h[:, C_ID:C_ID + 128]), r=[cst0], w=[identb])
            op("dve", lambda e: e.tensor_copy(Rb.h[:, :], cst0.h[:, C_R:C_R + 128]), r=[cst0], w=[Rb])
            op("dve", lambda e: e.tensor_copy(cbTb.h[:, :, :], cst0.h[:, C_CBT:C_CBT + 512].rearrange("p (a b) -> p a b", a=2)),
               r=[cst0], w=[cbTb])
            S.barrier()
        op("dve", lambda e: e.memset(epst.h[:, :], 1e-6), w=[epst])
        wst_i = [0]
        scr_i = [0]
        xi_i = [0]
        xst_i = [0]

        def load_w(dst, ncols, src_fn, stg=None):
            nchunk = dst.h.shape[1]
            if stg is None:
                stg = wst
            for c in range(nchunk):
                for b0 in range(0, ncols, 512):
                    bw = min(512, ncols - b0)
                    st = stg[wst_i[0] % len(stg)]
                    wst_i[0] += 1
                    S.dma("sp", st.h[:, 0:bw], src_fn(c)[:, b0:b0 + bw], st)
                    if wst_i[0] % 2 == 0:
                        op("act", lambda e: e.activation(out=dst.h[:, c, b0:b0 + bw], in_=st.h[:, 0:bw], func=AF.Copy), r=[st], w=[dst])
                    else:
                        op("dve", lambda e: e.tensor_copy(dst.h[:, c, b0:b0 + bw], st.h[:, 0:bw]), r=[st], w=[dst])

        def win_src(col0, ncols):
            return lambda c: w_in[c * 128:(c + 1) * 128, col0:col0 + ncols]

        def rmsnorm_T(src_fn, ntiles, dstT, gcol, xst, xn):
            for tt in range(ntiles):
                xs = xst[xst_i[0] % len(xst)]
                xnn = xn[xst_i[0] % len(xn)]
                xst_i[0] += 1
                k = tt % 4
                ssv, rsv = ss_v[k], rs_v[k]
                S.dma("sp", xs.h[:, :], src_fn(tt), xs)
                op("act", lambda e: e.activation(out=xnn.h[:, :], in_=xs.h[:, :], func=AF.Square,
                                                 accum_out=smallf.h[:, k:k + 1]), r=[xs], w=[xnn, ssv])
                op("act", lambda e: e.activation(out=smallf.h[:, 4 + k:5 + k], in_=smallf.h[:, k:k + 1], func=AF.Sqrt,
                                                 scale=1.0 / DM, bias=epst.h[:, 0:1]), r=[ssv, epst], w=[rsv])
                op("dve", lambda e: e.reciprocal(smallf.h[:, 4 + k:5 + k], smallf.h[:, 4 + k:5 + k]), r=[rsv], w=[rsv])
                op("dve", lambda e: e.tensor_scalar(out=xnn.h[:, :], in0=xs.h[:, :], scalar1=smallf.h[:, 4 + k:5 + k], scalar2=None,
                                                    op0=ALU.mult), r=[xs, rsv], w=[xnn])
                for c in range(8):
                    op("pe", lambda e: e.transpose(pT.h[:, c, :], xnn.h[:, c * 128:(c + 1) * 128], identb.h[:, :]),
                       r=[xnn, identb], w=[pT])
                op("dve", lambda e: e.tensor_tensor(out=dstT.h[:, :, tt * 128:(tt + 1) * 128], in0=pT.h[:, :, :],
                                                    in1=vec.h[:, gcol:gcol + 8].unsqueeze(2).to_broadcast([128, 8, 128]), op=ALU.mult),
                   r=[pT, vec], w=[dstT])

        ab_i = [0]

        def proj_fm(W, col0, rhsT, t0, n, m=128, ps=None):
            if ps is None:
                ps = pAB[ab_i[0] % 2]
                ab_i[0] += 1
            nchunk = W.h.shape[1]
            for c in range(nchunk):
                op("pe", lambda e: e.matmul(ps.h[0:m, 0:n], lhsT=W.h[:, c, col0:col0 + m], rhs=rhsT.h[:, c, t0:t0 + n],
                                            start=(c == 0), stop=(c == nchunk - 1)), r=[W, rhsT], w=[ps])
            return ps

        def proj_tm(W, col0, ncols, lhsT_, t0, ps=None):
            if ps is None:
                ps = pAB[ab_i[0] % 2]
                ab_i[0] += 1
            nchunk = W.h.shape[1]
            for c in range(nchunk):
                op("pe", lambda e: e.matmul(ps.h[:, 0:ncols], lhsT=lhsT_.h[:, c, t0:t0 + 128], rhs=W.h[:, c, col0:col0 + ncols],
                                            start=(c == 0), stop=(c == nchunk - 1)), r=[W, lhsT_], w=[ps])
            return ps

        def rope(ps, n, pos0, dst_ap, dst, scr, add_eng="pool"):
            if isinstance(scr, list):
                scr_i[0] += 1
                scr = scr[scr_i[0] % len(scr)]
            zb, t1, t2 = scr
            ck("kp")
            op("act", lambda e: e.activation(out=zb.h[:, 0:n], in_=ps.h[:, 0:n], func=AF.Copy), r=[ps], w=[zb])
            ck("kr0")
            op("pe", lambda e: e.matmul(pR.h[:, 0:n], lhsT=Rb.h[:, :], rhs=zb.h[:, 0:n], start=True, stop=True),
               r=[Rb, zb], w=[pR])
            ck("kr1")
            op("dve", lambda e: e.tensor_tensor(out=t1.h[:, 0:n], in0=ps.h[:, 0:n], in1=cosT.h[:, pos0:pos0 + n], op=ALU.mult),
               r=[ps, cosT], w=[t1])
            op("dve", lambda e: e.tensor_tensor(out=t2.h[:, 0:n], in0=pR.h[:, 0:n], in1=sinT.h[:, pos0:pos0 + n], op=ALU.mult),
               r=[pR, sinT], w=[t2])
            ck("kr2")
            if isinstance(dst_ap, tuple):
                op("pool", lambda e: e.tensor_tensor(out=dst_ap[0], in0=t1.h[0:64, 0:n], in1=t2.h[0:64, 0:n], op=ALU.add),
                   r=[t1, t2], w=[dst])
                op("pool", lambda e: e.tensor_tensor(out=dst_ap[1], in0=t1.h[64:128, 0:n], in1=t2.h[64:128, 0:n], op=ALU.add),
                   r=[t1, t2], w=[dst])
            else:
                op(add_eng, lambda e: e.tensor_tensor(out=dst_ap, in0=t1.h[:, 0:n], in1=t2.h[:, 0:n], op=ALU.add),
                   r=[t1, t2], w=[dst])

        def dump(name, t, ap, shape):
            if name not in dbg_out:
                return
            with ExitStack() as es:
                tmp = S.tile(es, "dbgtmp", [128, 512], F32)
                if len(shape) == 2:
                    pieces = [(ap[:, c0:min(c0 + 512, shape[1])], dbg_out[name][:, c0:min(c0 + 512, shape[1])], min(512, shape[1] - c0))
                              for c0 in range(0, shape[1], 512)]
                else:
                    pieces = [(ap[:, a, c0:min(c0 + 512, shape[2])], dbg_out[name][:, a, c0:min(c0 + 512, shape[2])], min(512, shape[2] - c0))
                              for a in range(shape[1]) for c0 in range(0, shape[2], 512)]
                for src, dst, n in pieces:
                    op("dve", lambda e: e.tensor_copy(tmp.h[:, 0:n], src), r=[t], w=[tmp])
                    S.dma("sp", dst, tmp.h[:, 0:n], tmp, load=False)
                S.barrier()

        def attn_out(acc_fn, sg, abt, aT, tok0, nheads, hd):
            pass

        def ck(name):
            if stop == name:
                S.barrier()
                S.dead = True

        try:
          for b in range(nb_run):
              with ExitStack() as st0:
                  xst = [S.tile(st0, "xst%d" % i, [128, 1024], F32) for i in range(4)]
                  xn = [S.tile(st0, "xn%d" % i, [128, 1024], BF16) for i in range(4)]
                  rmsnorm_T(lambda tt: x[b, tt * 128:(tt + 1) * 128, :], SEQ // 128, hT, 0, xst, xn)
                  S.barrier()
              if b == 0:
                  dump("hT", hT, hT.h[:, :, :], (128, 8, SEQ))
              ck("stage0")

              with ExitStack() as st_a:
                  aT_dsa = S.tile(st_a, "aT_dsa", [128, 4, SEQ], BF16)
                  with ExitStack() as st:
                      KT = S.tile(st, "KT", [128, 4, SEQ], BF16)
                      VA = S.tile(st, "VA", [128, 16, 8, 65], BF16)
                      wA = S.tile(st, "wA", [128, 8, 512], BF16)
                      wB = S.tile(st, "wB", [128, 8, 512], BF16)
                      wC = S.tile(st, "wC", [128, 8, 512], BF16)
                      wI = S.tile(st, "wI", [128, 8, 8], BF16)
                      ikT = S.tile(st, "ikT", [128, SEQ], BF16)
                      stk = ExitStack()
                      zb = S.tile(stk, "zb", [128, 512], BF16)
                      t1 = S.tile(stk, "t1", [128, 512], F32)
                      t2 = S.tile(stk, "t2", [128, 512], F32)
                      scr = (zb, t1, t2)
                      wst6 = wst + [S.tile(stk, "wstk%d" % i, [128, 512], F32) for i in range(4)]
                      load_w(wA, 512, win_src(OFF["dk"], 512), stg=wst6)
                      load_w(wB, 512, win_src(OFF["dv"], 512), stg=wst6)
                      load_w(wC, 64, win_src(OFF["ik"], 64), stg=wst6)
                      ck("kw")
                      op("pool", lambda e: e.tensor_copy(wC.h[:, :, 64:128], wC.h[:, :, 0:64]), r=[wC], w=[wC])
                      op("pool", lambda e: e.memset(VA.h[:, :, :, 64:65], 1.0), w=[VA])
                      ck("kw2")
                      for tg in range(4):
                          for pr in range(4):
                              ps = proj_fm(wA, pr * 128, hT, tg * 512, 512)
                              rope(ps, 512, tg * 512, KT.h[:, pr, tg * 512:(tg + 1) * 512], KT, scr)
                              ck("kr")
                          ps = proj_fm(wC, 0, hT, tg * 512, 512)
                          rope(ps, 512, tg * 512, ikT.h[:, tg * 512:(tg + 1) * 512], ikT, scr)
                          for t4 in range(4):
                              tt = tg * 4 + t4
                              ps = proj_tm(wB, 0, 512, hT, tt * 128)
                              op("act", lambda e: e.activation(out=VA.h[:, tt, :, 0:64],
                                                               in_=ps.h[:, :].rearrange("p (h d) -> p h d", h=8),
                                                               func=AF.Copy), r=[ps], w=[VA])
                      if b == 0:
                          dump("dKT", KT, KT.h[:, :, :], (128, 4, SEQ))
                          dump("ikT", ikT, ikT.h[:, :], (128, SEQ))
                      ck("kside")
                      S.barrier()
                      stk.close()
                      sgall = S.tile(st, "sgall", [128, 16, 512], BF16)
                      with ExitStack() as stq:
                          wst6 = wst + [S.tile(stq, "wstq%d" % i, [128, 512], F32) for i in range(4)]
                          load_w(wB, 512, win_src(OFF["dg"], 512), stg=wst6)
                          load_w(wA, 512, win_src(OFF["dq"], 512), stg=wst6)
                          load_w(wC, 512, win_src(OFF["iq"], 512), stg=wst6)
                          load_w(wI, 8, win_src(OFF["iw"], 8), stg=wst6)
                          for tt in range(16):
                              ps = proj_tm(wB, 0, 512, hT, tt * 128)
                              op("act", lambda e: e.activation(out=sgall.h[:, tt, :], in_=ps.h[:, :], func=AF.Silu), r=[ps], w=[sgall])
                          S.barrier()
                      zb = S.tile(st, "zbq", [128, 256], BF16)
                      t1 = S.tile(st, "t1q", [128, 256], F32)
                      t2 = S.tile(st, "t2q", [128, 256], F32)
                      scr = (zb, t1, t2)
                      accS = S.tile(st, "accS", [128, 8, 2, 65], F32)
                      rdv = S.tile(st, "rdv", [128, 8, 2], F32)
                      QT = [S.tile(st, "QT%d" % i, [128, 4, 2, 256], BF16) for i in range(2)]
                      for i in range(2):
                          op("pool", lambda e: e.memset(QT[i].h[:, :, :, :], 0.0), w=[QT[i]])
                      iqT = S.tile(st, "iqT", [128, 4, 256], BF16)
                      abt = [S.tile(st, "abt%d" % i, [128, 512], BF16) for i in range(2)]
                      rl = [S.tile(st, "rl%d" % i, [128, 512], F32) for i in range(2)]
                      sc = [S.tile(st, "sc%d" % i, [128, SEQ], F32) for i in range(2)]
                      mb = [[S.tile(st, "mb%d_%d" % (j, i), [128, SEQ], BF16) for i in range(2)] for j in range(2)]
                      cnts = S.tile(st, "cnts", [128, 4], F32)
                      cnt0 = cnts.view("c0")
                      cnt1 = cnts.view("c1")
                      cthr = cnts.view("thr")
                      PT = [S.tile(st, "PT%d" % i, [128, 2, 256], BF16) for i in range(2)]
                      iws = S.tile(st, "iws", [128, 2, 8], F32)
                      bis = S.tile(st, "bis", [128, 16], F32)
                      wk_all = S.tile(st, "wk_all", [128, 2, NITER], F32)
                      gem = S.tile(st, "gem", [128, 2], U32)
                      rd = S.tile(st, "rd", [128, 4], F32)

                      op("dve", lambda e: e.memset(bis.h[:, :], 0.0), w=[bis])

                      def X_chunks(m):
                          N = 256 * (m + 1)
                          tok0 = 256 * m
                          bi = m % 2
                          ch = []

                          def c_proj(pr):
                              ps = proj_fm(wA, pr * 128, hT, tok0, 256)
                              rope(ps, 256, tok0, (QT[bi].h[0:64, pr, 0, :], QT[bi].h[64:128, pr, 1, :]), QT[bi], scr)
                              ps = proj_fm(wC, pr * 128, hT, tok0, 256)
                              rope(ps, 256, tok0, iqT.h[:, pr, :], iqT, scr, add_eng="dve")
                          for pr in range(4):
                              ch.append(lambda pr=pr: c_proj(pr))

                          def c_gate(t):
                              ps = proj_tm(wI, 0, 8, hT, tok0 + t * 128)
                              op("dve", lambda e: e.tensor_scalar(out=iws.h[:, t, :], in0=ps.h[:, 0:8], scalar1=0.125 * (8 ** -0.5),
                                                                  scalar2=None, op0=ALU.mult), r=[ps], w=[iws])
                          ch.append(lambda: (c_gate(0), c_gate(1)))

                          def c_idx(t, k0):
                              kw = min(512, N - k0)
                              for g in range(8):
                                  p0 = (g % 2) * 64
                                  pss = [pA, pB, pR][xi_i[0] % 3]
                                  rlt = rl[xi_i[0] % 2]
                                  xi_i[0] += 1
                                  op("pe", lambda e: e.matmul(pss.h[:, 0:kw], lhsT=iqT.h[p0:p0 + 64, g // 2, t * 128:(t + 1) * 128],
                                                              rhs=ikT.h[p0:p0 + 64, k0:k0 + kw], start=True, stop=True),
                                     r=[iqT, ikT], w=[pss])
                                  op("act", lambda e: e.activation(out=rlt.h[:, 0:kw], in_=pss.h[:, 0:kw], func=AF.Relu),
                                     r=[pss], w=[rlt])
                                  if g == 0:
                                      op("dve", lambda e: e.tensor_scalar(out=sc[t].h[:, k0:k0 + kw], in0=rlt.h[:, 0:kw],
                                                                          scalar1=iws.h[:, t, 0:1], scalar2=None, op0=ALU.mult),
                                         r=[rlt, iws], w=[sc[t]])
                                  else:
                                      op("dve", lambda e: e.scalar_tensor_tensor(out=sc[t].h[:, k0:k0 + kw], in0=rlt.h[:, 0:kw],
                                                                                 scalar=iws.h[:, t, g:g + 1], in1=sc[t].h[:, k0:k0 + kw],
                                                                                 op0=ALU.mult, op1=ALU.add),
                                         r=[rlt, iws, sc[t]], w=[sc[t]])
                          for t in range(2):
                              for k0 in range(0, N, 512):
                                  ch.append(lambda t=t, k0=k0: c_idx(t, k0))

                          def c_pre():
                              if m == 0:
                                  op("dve", lambda e: e.memset(bis.h[:, 6:8], -1.0e29), w=[bis])
                              else:
                                  op("dve", lambda e: e.memset(cnts.h[:, 2:3], float(N - 256)), w=[cthr])
                                  op("dve", lambda e: e.memset(cnts.h[:, 3:4], float(N - 512)), w=[cthr])
                                  for t in range(2):
                                      op("dve", lambda e: e.tensor_reduce(out=bis.h[:, 10 + t:11 + t], in_=sc[t].h[:, 0:N], axis=AX.X, op=ALU.max,
                                                                          apply_absolute_value=True), r=[sc[t]], w=[bis])
                                  op("dve", lambda e: e.tensor_scalar(out=bis.h[:, 0:2], in0=bis.h[:, 10:12], scalar1=-1.0, scalar2=None, op0=ALU.mult),
                                     r=[bis], w=[bis])
                                  op("dve", lambda e: e.tensor_scalar(out=bis.h[:, 8:10], in0=bis.h[:, 10:12], scalar1=2.0, scalar2=None, op0=ALU.mult),
                                     r=[bis], w=[bis])
                                  op("dve", lambda e: e.tensor_tensor(out=wk_all.h[:, :, :],
                                                                      in0=bis.h[:, 8:10].unsqueeze(2).to_broadcast([128, 2, NITER]),
                                                                      in1=cst.h[:, C_PW - C_CBQ:C_PW - C_CBQ + NITER].unsqueeze(1).to_broadcast([128, 2, NITER]),
                                                                      op=ALU.mult), r=[bis, cst], w=[wk_all])
                              for t in range(2):
                                  op("dve", lambda e: e.tensor_tensor(out=sc[t].h[:, N - 256:N], in0=sc[t].h[:, N - 256:N],
                                                                      in1=cst.h[:, t * 256:(t + 1) * 256], op=ALU.add),
                                     r=[sc[t], cst], w=[sc[t]])
                          ch.append(c_pre)

                          def c_iter(it):
                              op("dve", lambda e: e.tensor_tensor(out=bis.h[:, 2:4], in0=bis.h[:, 0:2], in1=wk_all.h[:, :, it], op=ALU.add),
                                 r=[bis, wk_all], w=[bis])
                              op("act", lambda e: e.activation(out=mb[bi][1].h[:, 0:N], in_=sc[1].h[:, 0:N], func=AF.Sign, bias=bis.h[:, 3:4],
                                                               scale=-1.0, accum_out=cnts.h[:, 1:2]), r=[sc[1], bis], w=[mb[bi][1], cnt1])
                              op("dve", lambda e: e.tensor_scalar(out=mb[bi][0].h[:, 0:N], in0=sc[0].h[:, 0:N], scalar1=bis.h[:, 2:3],
                                                                  scalar2=None, op0=ALU.is_lt, op1=ALU.add,
                                                                  accum_out=cnts.h[:, 0:1]), r=[sc[0], bis], w=[mb[bi][0], cnt0])
                              op("dve", lambda e: e.tensor_tensor(out=gem.h[:, :], in0=cnts.h[:, 0:2], in1=cnts.h[:, 2:4], op=ALU.is_le),
                                 r=[cnt0, cnt1, cthr], w=[gem])
                              op("dve", lambda e: e.copy_predicated(bis.h[:, 0:2], gem.h[:, :], bis.h[:, 2:4]), r=[gem, bis], w=[bis])
                          if m > 0:
                              for it in range(NITER):
                                  ch.append(lambda it=it: c_iter(it))

                          def c_fin():
                              if m > 0:
                                  op("dve", lambda e: e.tensor_copy(bis.h[:, 6:8], bis.h[:, 0:2]), r=[bis], w=[bis])
                              for t in range(2):
                                  op("dve", lambda e: e.tensor_scalar(out=mb[bi][t].h[:, 0:N], in0=sc[t].h[:, 0:N], scalar1=bis.h[:, 6 + t:7 + t],
                                                                      scalar2=NEG, op0=ALU.is_lt, op1=ALU.mult), r=[sc[t], bis], w=[mb[bi][t]])
                          ch.append(c_fin)
                          return ch

                      for f in X_chunks(0):
                          f()
                      dsa_tail = [None]
                      for m in range(8):
                          N = 256 * (m + 1)
                          tok0 = 256 * m
                          bi = m % 2
                          nkt = N // 128
                          steps = [(h, jj) for h in range(8) for jj in range(nkt // 2)]
                          xc = X_chunks(m + 1) if m + 1 < 8 else []

                          def d_qk(i):
                              h, jj = steps[i]
                              p0 = (h % 2) * 64
                              pr = h // 2
                              pss = pS[i % 2]
                              for jl in range(2):
                                  j = jj * 2 + jl
                                  op("pe", lambda e: e.matmul(pss.h[:, jl * 256:(jl + 1) * 256], lhsT=KT.h[:, pr, j * 128:(j + 1) * 128],
                                                              rhs=QT[bi].h[:, pr, h % 2, :], start=True, stop=False), r=[KT, QT[bi]], w=[pss])
                                  for t in range(2):
                                      op("pe", lambda e: e.matmul(pss.h[:, jl * 256 + t * 128:jl * 256 + (t + 1) * 128],
                                                                  lhsT=mb[bi][t].h[:, j * 128:(j + 1) * 128], rhs=identb.h[:, :],
                                                                  start=False, stop=(t == 1)), r=[mb[bi][t], identb], w=[pss])

                          def d_rest(i):
                              h, jj = steps[i]
                              pss = pS[i % 2]
                              ptt = PT[i % 2]
                              acc = pV[h % 2]
                              op("act", lambda e: e.activation(out=ptt.h[:, :, :], in_=pss.h[:, :].rearrange("p (a b) -> p a b", a=2),
                                                               func=AF.Exp, scale=0.125), r=[pss], w=[ptt])
                              for jl in range(2):
                                  j = jj * 2 + jl
                                  for t in range(2):
                                      op("pe", lambda e: e.matmul(acc.h[:, t * 65:(t + 1) * 65], lhsT=ptt.h[:, jl, t * 128:(t + 1) * 128],
                                                                  rhs=VA.h[:, j, h, :], start=(j == 0 and t == 0), stop=(j == nkt - 1),
                                                                  skip_group_check=True),
                                         r=[ptt, VA], w=[acc])
                              if jj == nkt // 2 - 1:
                                  op("act", lambda e: e.activation(out=accS.h[:, h, :, :], in_=acc.h[:, 0:130].rearrange("p (t d) -> p t d", t=2),
                                                                   func=AF.Copy), r=[acc], w=[accS])

                          d_qk(0)
                          if dsa_tail[0] is not None:
                              dsa_tail[0]()
                          done = 0
                          for i in range(len(steps)):
                              if i + 1 < len(steps):
                                  d_qk(i + 1)
                              d_rest(i)
                              target = ((i + 1) * len(xc) + len(steps) - 1) // len(steps)
                              while done < target:
                                  xc[done]()
                                  done += 1
                          def make_dsa_tail(tok0=tok0):
                              def tail():
                                  op("dve", lambda e: e.reciprocal(rdv.h[:, :, :], accS.h[:, :, :, 64]), r=[accS], w=[rdv])
                                  for t in range(2):
                                      op("dve", lambda e: e.tensor_tensor(out=rl[t].h[:, :].rearrange("p (h d) -> p h d", h=8), in0=accS.h[:, :, t, 0:64],
                                                                          in1=rdv.h[:, :, t].unsqueeze(2).to_broadcast([128, 8, 64]), op=ALU.mult),
                                         r=[accS, rdv], w=[rl[t]])
                                      op("pool", lambda e: e.tensor_tensor(out=abt[t].h[:, :], in0=rl[t].h[:, :], in1=sgall.h[:, (tok0 // 128) + t, :], op=ALU.mult),
                                         r=[rl[t], sgall], w=[abt[t]])
                                      for c in range(4):
                                          op("pe", lambda e: e.transpose(pT.h[:, c, :], abt[t].h[:, c * 128:(c + 1) * 128], identb.h[:, :]),
                                             r=[abt[t], identb], w=[pT])
                                      op("dve", lambda e: e.tensor_copy(aT_dsa.h[:, :, tok0 + t * 128:tok0 + (t + 1) * 128], pT.h[:, 0:4, :]),
                                         r=[pT], w=[aT_dsa])
                              return tail
                          dsa_tail[0] = make_dsa_tail()
                      dsa_tail[0]()
                      S.barrier()
                  if b == 0:
                      dump("aT_dsa", aT_dsa, aT_dsa.h[:, :, :], (128, 4, SEQ))
                  S.barrier()
                  ck("dsa")
                  with ExitStack() as st_b:
                      aT_moba = S.tile(st_b, "aT_moba", [128, 4, SEQ], BF16)
                      with ExitStack() as st:
                          KT = S.tile(st, "KT", [128, 4, SEQ], BF16)
                          VA = S.tile(st, "VA", [128, 16, 8, 65], BF16)
                          wA = S.tile(st, "wA", [128, 8, 512], BF16)
                          wB = S.tile(st, "wB", [128, 8, 512], BF16)
                          scr = [(S.tile(st, "zb%d" % i, [128, 512], BF16), S.tile(st, "t1_%d" % i, [128, 512], F32),
                                  S.tile(st, "t2_%d" % i, [128, 512], F32)) for i in range(2)]
                          QT = [S.tile(st, "QT%d" % i, [128, 4, 2, 256], BF16) for i in range(2)]
                          QTg = S.tile(st, "QTg", [128, 4, 256], BF16)
                          for i in range(2):
                              op("pool", lambda e: e.memset(QT[i].h[:, :, :, :], 0.0), w=[QT[i]])
                          sgall = S.tile(st, "sgall", [128, 16, 512], BF16)
                          yb = [S.tile(st, "yb%d" % i, [128, 512], F32) for i in range(2)]
                          abt = [S.tile(st, "abt%d" % i, [128, 512], BF16) for i in range(2)]
                          PT = [S.tile(st, "PT%d" % i, [128, 2, 256], BF16) for i in range(3)]
                          pS3 = [pS[0], pS[1], pB]
                          kmf = S.tile(st, "kmf", [128, 4, 8], F32)
                          kmT = S.tile(st, "kmT", [128, 4, 16], BF16)
                          gt = S.tile(st, "gt", [128, 2, 8, 8], F32)
                          cmpb = S.tile(st, "cmpb", [128, 16, 8, 8], F32)
                          rank = S.tile(st, "rank", [128, 16, 8], F32)
                          sel = [S.tile(st, "sel%d" % i, [128, 2, 8, 8], F32) for i in range(2)]
                          accs = [S.tile(st, "accs%d" % i, [128, 8, 65], F32) for i in range(2)]
                          rdv = S.tile(st, "rdv", [128, 2, 8], F32)
                          wst6 = wst + [S.tile(st, "wstm%d" % i, [128, 512], F32) for i in range(4)]
                          load_w(wA, 512, win_src(OFF["mk"], 512), stg=wst6)
                          load_w(wB, 512, win_src(OFF["mv"], 512), stg=wst6)
                          op("pool", lambda e: e.memset(VA.h[:, :, :, 64:65], 1.0), w=[VA])
                          for tg in range(4):
                              for pr in range(4):
                                  ps = proj_fm(wA, pr * 128, hT, tg * 512, 512)
                                  rope(ps, 512, tg * 512, KT.h[:, pr, tg * 512:(tg + 1) * 512], KT, scr)
                              for t4 in range(4):
                                  tt = tg * 4 + t4
                                  ps = proj_tm(wB, 0, 512, hT, tt * 128)
                                  op("act", lambda e: e.activation(out=VA.h[:, tt, :, 0:64],
                                                                   in_=ps.h[:, :].rearrange("p (h d) -> p h d", h=8),
                                                                   func=AF.Copy), r=[ps], w=[VA])
                          ck("mk")
                          for pr in range(4):
                              op("dve", lambda e: e.tensor_reduce(out=kmf.h[:, pr, :], in_=KT.h[:, pr, :].rearrange("p (n k) -> p n k", n=8),
                                                                  axis=AX.X, op=ALU.add), r=[KT], w=[kmf])
                          op("dve", lambda e: e.memset(kmT.h[:, :, :], 0.0), w=[kmT])
                          op("dve", lambda e: e.tensor_scalar(out=kmT.h[0:64, :, 0:8], in0=kmf.h[0:64, :, :], scalar1=1.0 / 256.0, scalar2=None,
                                                              op0=ALU.mult), r=[kmf], w=[kmT])
                          op("dve", lambda e: e.tensor_scalar(out=kmT.h[64:128, :, 8:16], in0=kmf.h[64:128, :, :], scalar1=1.0 / 256.0, scalar2=None,
                                                              op0=ALU.mult), r=[kmf], w=[kmT])
                          ck("mkm")
                          load_w(wB, 512, win_src(OFF["mg"], 512), stg=wst6)
                          load_w(wA, 512, win_src(OFF["mq"], 512), stg=wst6)
                          for tt in range(16):
                              ps = proj_tm(wB, 0, 512, hT, tt * 128)
                              op("act", lambda e: e.activation(out=sgall.h[:, tt, :], in_=ps.h[:, :], func=AF.Silu), r=[ps], w=[sgall])
                          def MX_chunks(m):
                              tok0 = 256 * m
                              bi = m % 2
                              ch = []

                              def c_proj(pr):
                                  ps = proj_fm(wA, pr * 128, hT, tok0, 256, ps=pA)
                                  rope(ps, 256, tok0, (QT[bi].h[0:64, pr, 0, :], QT[bi].h[64:128, pr, 1, :]), QT[bi], scr)
                              for pr in range(4):
                                  ch.append(lambda pr=pr: c_proj(pr))

                              def c_sel0():
                                  if m <= 3:
                                      op("dve", lambda e: e.memset(sel[bi].h[:, :, :, :], 1.0), w=[sel[bi]])
                                      return
                                  op("pool", lambda e: e.tensor_tensor(out=QTg.h[:, :, :], in0=QT[bi].h[:, :, 0, :], in1=QT[bi].h[:, :, 1, :], op=ALU.add),
                                     r=[QT[bi]], w=[QTg])
                                  for t in range(2):
                                      for pr in range(4):
                                          op("pe", lambda e: e.matmul(pR.h[:, t * 64 + pr * 16:t * 64 + pr * 16 + 16],
                                                                      lhsT=QTg.h[:, pr, t * 128:(t + 1) * 128],
                                                                      rhs=kmT.h[:, pr, :], start=True, stop=True),
                                             r=[QTg, kmT], w=[pR])
                                  op("dve", lambda e: e.tensor_copy(gt.h[:, :, :, :], pR.h[:, 0:128].rearrange("p (t h n) -> p t h n", t=2, h=8)),
                                     r=[pR], w=[gt])
                                  op("dve", lambda e: e.memset(gt.h[:, :, :, m:8], NEGF), w=[gt])
                              ch.append(c_sel0)

                              def c_sel1():
                                  g3 = gt.h[:, :, :, :].rearrange("p t h n -> p (t h) n")
                                  op("dve", lambda e: e.tensor_tensor(out=cmpb.h[:, :, :, :], in0=g3.unsqueeze(2).to_broadcast([128, 16, 8, 8]),
                                                                      in1=g3.unsqueeze(3).to_broadcast([128, 16, 8, 8]), op=ALU.is_gt),
                                     r=[gt], w=[cmpb])
                                  op("dve", lambda e: e.tensor_reduce(out=rank.h[:, :, :], in_=cmpb.h[:, :, :, :], axis=AX.X, op=ALU.add),
                                     r=[cmpb], w=[rank])
                                  op("dve", lambda e: e.tensor_scalar(out=sel[bi].h[:, :, :, :].rearrange("p t h n -> p (t h) n"), in0=rank.h[:, :, :],
                                                                      scalar1=3.0, scalar2=None, op0=ALU.is_lt), r=[rank], w=[sel[bi]])
                                  op("dve", lambda e: e.memset(sel[bi].h[:, :, :, m:m + 1], 1.0), w=[sel[bi]])
                              if m > 3:
                                  ch.append(c_sel1)
                              return ch

                          for f in MX_chunks(0):
                              f()
                          pending_tail = [None]
                          for m in range(8):
                              tok0 = 256 * m
                              bi = m % 2
                              xc = MX_chunks(m + 1) if m + 1 < 8 else []
                              steps = [(h, n) for h in range(8) for n in range(m + 1)]

                              def m_qk(i):
                                  h, n = steps[i]
                                  pr = h // 2
                                  pss = pS3[i % 3]
                                  for jl in range(2):
                                      j = 2 * n + jl
                                      op("pe", lambda e: e.matmul(pss.h[:, jl * 256:(jl + 1) * 256], lhsT=KT.h[:, pr, j * 128:(j + 1) * 128],
                                                                  rhs=QT[bi].h[:, pr, h % 2, :], start=True, stop=(n < m)), r=[KT, QT[bi]], w=[pss])
                                      if n == m:
                                          op("pe", lambda e: e.matmul(pss.h[:, jl * 256:(jl + 1) * 256], lhsT=identb.h[:, :], rhs=cbTb.h[:, jl, :],
                                                                      start=False, stop=True), r=[identb, cbTb], w=[pss])

                              def m_rest(i):
                                  h, n = steps[i]
                                  pss = pS3[i % 3]
                                  ptt = PT[i % 3]
                                  acc = pV[i % 2]
                                  op("act", lambda e: e.activation(out=ptt.h[:, :, :], in_=pss.h[:, :].rearrange("p (a b) -> p a b", a=2),
                                                                   func=AF.Exp, scale=0.125), r=[pss], w=[ptt])
                                  for jl in range(2):
                                      j = 2 * n + jl
                                      for t in range(2):
                                          op("pe", lambda e: e.matmul(acc.h[:, t * 65:(t + 1) * 65], lhsT=ptt.h[:, jl, t * 128:(t + 1) * 128],
                                                                      rhs=VA.h[:, j, h, :], start=(jl == 0 and t == 0), stop=(jl == 1),
                                                                      skip_group_check=True), r=[ptt, VA], w=[acc])
                                  for t in range(2):
                                      if n == 0:
                                          op("dve", lambda e: e.tensor_scalar(out=accs[t].h[:, h, :], in0=acc.h[:, t * 65:(t + 1) * 65],
                                                                              scalar1=sel[bi].h[:, t, h, 0:1], scalar2=None, op0=ALU.mult),
                                             r=[acc, sel[bi]], w=[accs[t]])
                                      else:
                                          op("dve", lambda e: e.scalar_tensor_tensor(out=accs[t].h[:, h, :], in0=acc.h[:, t * 65:(t + 1) * 65],
                                                                                     scalar=sel[bi].h[:, t, h, n:n + 1], in1=accs[t].h[:, h, :],
                                                                                     op0=ALU.mult, op1=ALU.add),
                                             r=[acc, sel[bi], accs[t]], w=[accs[t]])

                              m_qk(0)
                              if len(steps) > 1:
                                  m_qk(1)
                              if pending_tail[0] is not None:
                                  pending_tail[0]()
                              done = 0
                              for i in range(len(steps)):
                                  if i + 2 < len(steps):
                                      m_qk(i + 2)
                                  m_rest(i)
                                  target = ((i + 1) * len(xc) + len(steps) - 1) // len(steps)
                                  while done < target:
                                      xc[done]()
                                      done += 1
                              def make_tail(tok0=tok0, bi=bi):
                                  def tail():
                                      for t in range(2):
                                          op("dve", lambda e: e.reciprocal(rdv.h[:, t, :], accs[t].h[:, :, 64]), r=[accs[t]], w=[rdv])
                                          op("dve", lambda e: e.tensor_tensor(out=yb[t].h[:, :].rearrange("p (h d) -> p h d", h=8), in0=accs[t].h[:, :, 0:64],
                                                                              in1=rdv.h[:, t, :].unsqueeze(2).to_broadcast([128, 8, 64]), op=ALU.mult),
                                             r=[accs[t], rdv], w=[yb[t]])
                                          op("pool", lambda e: e.tensor_tensor(out=abt[t].h[:, :], in0=yb[t].h[:, :], in1=sgall.h[:, (tok0 // 128) + t, :], op=ALU.mult),
                                             r=[yb[t], sgall], w=[abt[t]])
                                          for c in range(4):
                                              op("pe", lambda e: e.transpose(pT.h[:, c, :], abt[t].h[:, c * 128:(c + 1) * 128], identb.h[:, :]),
                                                 r=[abt[t], identb], w=[pT])
                                          op("dve", lambda e: e.tensor_copy(aT_moba.h[:, :, tok0 + t * 128:tok0 + (t + 1) * 128], pT.h[:, 0:4, :]),
                                             r=[pT], w=[aT_moba])
                                  return tail
                              pending_tail[0] = make_tail()
                          pending_tail[0]()
                          S.barrier()
                      if b == 0:
                          dump("aT_moba", aT_moba, aT_moba.h[:, :, :], (128, 4, SEQ))
                      S.barrier()
                      ck("moba")
                      with ExitStack() as st_c:
                          aT_x = S.tile(st_c, "aT_x", [128, 4, SEQ], BF16)
                          with ExitStack() as st:
                              memT = S.tile(st, "memT", [128, 8, MEM], BF16)
                              wK = S.tile(st, "wK", [128, 8, 1024], BF16)
                              xkT = S.tile(st, "xkT", [128, 4, MEM], BF16)
                              xva = S.tile(st, "xva", [128, 2, 4, 129], BF16)
                              wA = S.tile(st, "wA", [128, 8, 512], BF16)
                              wB = S.tile(st, "wB", [128, 8, 512], BF16)
                              xqT2 = [S.tile(st, "xqT%d" % i, [128, 4, 256], BF16) for i in range(2)]
                              sg2 = [[S.tile(st, "sg%d_%d" % (j, i), [128, 512], F32) for i in range(2)] for j in range(2)]
                              yb2 = [[S.tile(st, "yb%d_%d" % (j, i), [128, 512], F32) for i in range(2)] for j in range(2)]
                              abt = [S.tile(st, "abt%d" % i, [128, 512], BF16) for i in range(2)]
                              PT = [S.tile(st, "PT%d" % i, [128, 2, 256], BF16) for i in range(2)]
                              rd = S.tile(st, "rd", [128, 4], F32)
                              xst = [S.tile(st, "xst%d" % i, [128, 1024], F32) for i in range(2)]
                              xn = [S.tile(st, "xn%d" % i, [128, 1024], BF16) for i in range(2)]
                              rmsnorm_T(lambda tt: mem[b, tt * 128:(tt + 1) * 128, :], MEM // 128, memT, 8, xst, xn)
                              wst6 = wst + [S.tile(st, "wstx%d" % i, [128, 512], F32) for i in range(4)]
                              load_w(wK, 1024, lambda c: w_kv[c * 128:(c + 1) * 128, :], stg=wst6)
                              op("pool", lambda e: e.memset(xva.h[:, :, :, 128:129], 1.0), w=[xva])
                              for h in range(4):
                                  ps = proj_fm(wK, h * 128, memT, 0, MEM)
                                  op("act", lambda e: e.activation(out=xkT.h[:, h, :], in_=ps.h[:, 0:MEM], func=AF.Copy), r=[ps], w=[xkT])
                              for mt in range(2):
                                  ps = proj_tm(wK, 512, 512, memT, mt * 128)
                                  op("act", lambda e: e.activation(out=xva.h[:, mt, :, 0:128], in_=ps.h[:, :].rearrange("p (h d) -> p h d", h=4),
                                                                   func=AF.Copy), r=[ps], w=[xva])
                              load_w(wA, 512, win_src(OFF["xq"], 512), stg=wst6)
                              load_w(wB, 512, win_src(OFF["xg"], 512), stg=wst6)
                              def x_proj(m):
                                  tok0 = 256 * m
                                  bi = m % 2
                                  for h in range(4):
                                      ps = proj_fm(wA, h * 128, hT, tok0, 256)
                                      op("act", lambda e: e.activation(out=xqT2[bi].h[:, h, :], in_=ps.h[:, 0:256], func=AF.Copy), r=[ps], w=[xqT2[bi]])
                                  for t in range(2):
                                      ps = proj_tm(wB, 0, 512, hT, tok0 + t * 128)
                                      op("act", lambda e: e.activation(out=sg2[bi][t].h[:, :], in_=ps.h[:, :], func=AF.Silu), r=[ps], w=[sg2[bi][t]])

                              def x_qk(m, h):
                                  bi = m % 2
                                  pss = pS[h % 2]
                                  for mt in range(2):
                                      op("pe", lambda e: e.matmul(pss.h[:, mt * 256:(mt + 1) * 256], lhsT=xkT.h[:, h, mt * 128:(mt + 1) * 128],
                                                                  rhs=xqT2[bi].h[:, h, :], start=True, stop=True), r=[xkT, xqT2[bi]], w=[pss])

                              def x_rest(m, h):
                                  bi = m % 2
                                  pss = pS[h % 2]
                                  ptt = PT[h % 2]
                                  acc = pV[h % 2]
                                  op("act", lambda e: e.activation(out=ptt.h[:, :, :], in_=pss.h[:, :].rearrange("p (a b) -> p a b", a=2),
                                                                   func=AF.Exp, scale=128.0 ** -0.5), r=[pss], w=[ptt])
                                  for mt in range(2):
                                      for t in range(2):
                                          op("pe", lambda e: e.matmul(acc.h[:, t * 129:(t + 1) * 129], lhsT=ptt.h[:, mt, t * 128:(t + 1) * 128],
                                                                      rhs=xva.h[:, mt, h, :], start=(mt == 0 and t == 0), stop=(mt == 1),
                                                                      skip_group_check=True), r=[ptt, xva], w=[acc])
                                  for t in range(2):
                                      op("dve", lambda e: e.reciprocal(rd.h[:, t:t + 1], acc.h[:, t * 129 + 128:t * 129 + 129]), r=[acc], w=[rd])
                                      op("dve", lambda e: e.tensor_scalar(out=yb2[bi][t].h[:, h * 128:(h + 1) * 128], in0=acc.h[:, t * 129:t * 129 + 128],
                                                                          scalar1=rd.h[:, t:t + 1], scalar2=None, op0=ALU.mult),
                                         r=[acc, rd], w=[yb2[bi][t]])

                              def x_tail(m):
                                  tok0 = 256 * m
                                  bi = m % 2
                                  for t in range(2):
                                      op("pool", lambda e: e.tensor_tensor(out=abt[t].h[:, :], in0=yb2[bi][t].h[:, :], in1=sg2[bi][t].h[:, :], op=ALU.mult),
                                         r=[yb2[bi][t], sg2[bi][t]], w=[abt[t]])
                                      for c in range(4):
                                          op("pe", lambda e: e.transpose(pT.h[:, c, :], abt[t].h[:, c * 128:(c + 1) * 128], identb.h[:, :]),
                                             r=[abt[t], identb], w=[pT])
                                      op("dve", lambda e: e.tensor_copy(aT_x.h[:, :, tok0 + t * 128:tok0 + (t + 1) * 128], pT.h[:, 0:4, :]),
                                         r=[pT], w=[aT_x])

                              x_proj(0)
                              for m in range(8):
                                  x_qk(m, 0)
                                  if m > 0:
                                      x_tail(m - 1)
                                  if m + 1 < 8:
                                      x_proj(m + 1)
                                  for h in range(4):
                                      if h + 1 < 4:
                                          x_qk(m, h + 1)
                                      x_rest(m, h)
                              x_tail(7)
                              S.barrier()
                          if b == 0:
                              dump("aT_x", aT_x, aT_x.h[:, :, :], (128, 4, SEQ))
                          S.barrier()
                          ck("cross")
                          with ExitStack() as st:
                              aTs = [aT_moba, aT_dsa, aT_x]
                              uT = S.tile(st, "uT", [128, 8, SEQ], BF16)
                              wO = S.tile(st, "wO", [128, 8, 1024], BF16)
                              gfb = S.tile(st, "gfb", [128, DM], F32)
                              stw = ExitStack()
                              w4 = [S.tile(stw, "w4st%d" % i, [128, 1024], F32) for i in range(2)]
                              wG = [S.tile(stw, "wG%d" % i, [128, 8, 384], BF16) for i in range(2)]
                              wU = [[S.tile(stw, "wU%d_%d" % (j, i), [128, 4, 128], BF16) for i in range(3)] for j in range(2)]
                              gsbs = [S.tile(stw, "gsb%d" % i, [128, 512], F32) for i in range(2)]
                              uaccs = [S.tile(stw, "uacc%d" % i, [128, 512], F32) for i in range(2)]
                              tmpus = [S.tile(stw, "tmpu%d" % i, [128, 512], F32) for i in range(2)]
                              pQ = [pA, pB, pS[0], pS[1]]
                              q_i = [0]
                              w4_i = [0]
                              S.dma("sp", gfb.h[:, :], gfin[:, :], gfb)

                              def load_fo(fo):
                                  bi = fo % 2
                                  for mi in range(3):
                                      stg = w4[w4_i[0] % 2]
                                      w4_i[0] += 1
                                      c0 = OFF["gl"] + mi * 1024 + fo * 128
                                      S.dma("sp", stg.h[:, :].rearrange("p (c n) -> p c n", c=8),
                                            w_in[:, c0:c0 + 128].rearrange("(c p) n -> p c n", p=128), stg)
                                      op("act", lambda e: e.activation(out=wG[bi].h[:, :, mi * 128:(mi + 1) * 128],
                                                                       in_=stg.h[:, :].rearrange("p (c n) -> p c n", c=8), func=AF.Copy),
                                         r=[stg], w=[wG[bi]])
                                      stg = w4[w4_i[0] % 2]
                                      w4_i[0] += 1
                                      S.dma("sp", stg.h[:, 0:512].rearrange("p (c n) -> p c n", c=4),
                                            w_up[mi, :, fo * 128:(fo + 1) * 128].rearrange("(c p) n -> p c n", p=128), stg)
                                      op("dve", lambda e: e.tensor_copy(wU[bi][mi].h[:, :, :],
                                                                        stg.h[:, 0:512].rearrange("p (c n) -> p c n", c=4)), r=[stg], w=[wU[bi][mi]])

                              load_fo(0)
                              for fo in range(8):
                                  bi = fo % 2
                                  if fo + 1 < 8:
                                      load_fo(fo + 1)
                                  else:
                                      for c in range(8):
                                          stg = w4[w4_i[0] % 2]
                                          w4_i[0] += 1
                                          S.dma("sp", stg.h[:, :], w_out[c * 128:(c + 1) * 128, :], stg)
                                          if c % 2 == 0:
                                              op("act", lambda e: e.activation(out=wO.h[:, c, :], in_=stg.h[:, :], func=AF.Copy), r=[stg], w=[wO])
                                          else:
                                              op("dve", lambda e: e.tensor_copy(wO.h[:, c, :], stg.h[:, :]), r=[stg], w=[wO])
                                  for tg in range(4):
                                      uacc = uaccs[tg % 2]
                                      for mi in range(3):
                                          gsb = gsbs[q_i[0] % 2]
                                          tmpu = tmpus[q_i[0] % 2]
                                          psg = proj_fm(wG[bi], mi * 128, hT, tg * 512, 512, ps=pQ[(2 * q_i[0]) % 4])
                                          op("act", lambda e: e.activation(out=gsb.h[:, :], in_=psg.h[:, :], func=AF.Sigmoid,
                                                                           bias=vec.h[:, 16 + mi * 8 + fo:17 + mi * 8 + fo]), r=[psg, vec], w=[gsb])
                                          psy = proj_fm(wU[bi][mi], 0, aTs[mi], tg * 512, 512, ps=pQ[(2 * q_i[0] + 1) % 4])
                                          q_i[0] += 1
                                          if mi == 0:
                                              op("dve", lambda e: e.tensor_tensor(out=uacc.h[:, :], in0=psy.h[:, :], in1=gsb.h[:, :], op=ALU.mult),
                                                 r=[psy, gsb], w=[uacc])
                                          else:
                                              op("dve", lambda e: e.tensor_tensor(out=tmpu.h[:, :], in0=psy.h[:, :], in1=gsb.h[:, :], op=ALU.mult),
                                                 r=[psy, gsb], w=[tmpu])
                                              if mi == 1:
                                                  op("pool", lambda e: e.tensor_tensor(out=uacc.h[:, :], in0=uacc.h[:, :], in1=tmpu.h[:, :], op=ALU.add),
                                                     r=[uacc, tmpu], w=[uacc])
                                              else:
                                                  op("pool", lambda e: e.tensor_tensor(out=uT.h[:, fo, tg * 512:(tg + 1) * 512], in0=uacc.h[:, :],
                                                                                       in1=tmpu.h[:, :], op=ALU.add), r=[uacc, tmpu], w=[uT])
                              S.barrier()
                              stw.close()
                              xst = [S.tile(st, "xst%d" % i, [128, 1024], F32) for i in range(4)]
                              xn = [S.tile(st, "xn%d" % i, [128, 1024], BF16) for i in range(2)]
                              for tt in range(SEQ // 128):
                                  xs = xst[xst_i[0] % len(xst)]
                                  xnn = xn[xst_i[0] % len(xn)]
                                  xst_i[0] += 1
                                  k = tt % 4
                                  ssv, rsv = ss_v[k], rs_v[k]
                                  S.dma("sp", xs.h[:, :], x[b, tt * 128:(tt + 1) * 128, :], xs)
                                  for half in range(2):
                                      ps = proj_tm(wO, half * 512, 512, uT, tt * 128)
                                      op("dve", lambda e: e.tensor_tensor(out=xs.h[:, half * 512:(half + 1) * 512], in0=ps.h[:, :],
                                                                          in1=xs.h[:, half * 512:(half + 1) * 512], op=ALU.add), r=[ps, xs], w=[xs])
                                  op("act", lambda e: e.activation(out=xnn.h[:, :], in_=xs.h[:, :], func=AF.Square,
                                                                   accum_out=smallf.h[:, k:k + 1]), r=[xs], w=[xnn, ssv])
                                  op("act", lambda e: e.activation(out=smallf.h[:, 4 + k:5 + k], in_=smallf.h[:, k:k + 1], func=AF.Sqrt,
                                                                   scale=1.0 / DM, bias=epst.h[:, 0:1]), r=[ssv, epst], w=[rsv])
                                  op("dve", lambda e: e.reciprocal(smallf.h[:, 4 + k:5 + k], smallf.h[:, 4 + k:5 + k]), r=[rsv], w=[rsv])
                                  op("dve", lambda e: e.scalar_tensor_tensor(out=xs.h[:, :], in0=xs.h[:, :], scalar=smallf.h[:, 4 + k:5 + k],
                                                                             in1=gfb.h[:, :], op0=ALU.mult, op1=ALU.mult),
                                     r=[xs, rsv, gfb], w=[xs])
                                  S.dma("pool", y[b, tt * 128:(tt + 1) * 128, :], xs.h[:, :], xs, load=False)
                              S.barrier()
        except StopBuild:
            pass
        S.dead = False
        S.finish()
    return nc


def rope_tables():
    half = 32
    inv = np.power(np.float32(10000.0), -np.arange(half, dtype=np.float32) * np.float32(2.0) / np.float32(64)).astype(np.float32)
    pos = np.arange(SEQ, dtype=np.float32)
    ang = (pos[:, None] * inv[None, :]).astype(np.float32)
    cos = np.cos(ang).astype(np.float32).T
    sin = np.sin(ang).astype(np.float32).T
    cosT = np.tile(cos, (4, 1))
    sinT = np.tile(sin, (4, 1))
    return np.ascontiguousarray(np.concatenate([cosT, sinT], axis=1))


def const_tables():
    c = np.zeros((128, C_END), np.float32)
    c[:, C_ID:C_ID + 128] = np.eye(128, dtype=np.float32)
    R = np.zeros((128, 128), np.float32)
    for hb in (0, 64):
        for d in range(32):
            R[hb + d + 32, hb + d] = -1.0
            R[hb + d, hb + d + 32] = 1.0
    c[:, C_R:C_R + 128] = R
    k = np.arange(128)[:, None]
    q = np.arange(256)[None, :]
    c[:, C_CBT:C_CBT + 256] = np.where(k <= q, 0.0, NEG)
    c[:, C_CBT + 256:C_CBT + 512] = np.where(k + 128 <= q, 0.0, NEG)
    qq = np.arange(128)[:, None]
    kk = np.arange(256)[None, :]
    c[:, C_CBQ:C_CBQ + 256] = np.where(kk <= qq, 0.0, NEGF)
    c[:, C_CBQ + 256:C_CBQ + 512] = np.where(kk <= qq + 128, 0.0, NEGF)
    c[:, C_PW:C_PW + NITER] = (0.5 ** np.arange(1, NITER + 1, dtype=np.float64)).astype(np.float32)[None, :]
    return c


def make_in_maps(x, mem, g_in, w_in, b_merge, g_mem, w_mem_kv, w_up_moba, w_up_dsa, w_up_cross, w_out, g_final):
    f = lambda a: np.ascontiguousarray(np.asarray(a, dtype=np.float32))
    x, mem = f(x), f(mem)
    vecs = np.zeros((128, 40), np.float32)
    vecs[:, 0:8] = f(g_in)[0].reshape(8, 128).T
    vecs[:, 8:16] = f(g_mem)[0].reshape(8, 128).T
    vecs[:, 16:40] = f(b_merge)[0].reshape(24, 128).T
    gfin = np.ascontiguousarray(np.broadcast_to(f(g_final)[None, :], (128, DM)))
    wup = np.ascontiguousarray(np.stack([f(w_up_moba)[0], f(w_up_dsa)[0], f(w_up_cross)[0]], axis=0))
    shared = dict(w_in=f(w_in)[0], w_kv=f(w_mem_kv)[0], w_up=wup, w_out=f(w_out)[0], vecs=vecs, gfin=gfin,
                  ropet=rope_tables(), consts=const_tables())
    maps = []
    for c in range(NCORES):
        d = dict(shared)
        d["x"] = np.ascontiguousarray(x[c * NB:(c + 1) * NB])
        d["mem"] = np.ascontiguousarray(mem[c * NB:(c + 1) * NB])
        maps.append(d)
    return maps


def kernel(**inputs):
    nc = build()
    maps = make_in_maps(**inputs)
    res = run_bass_kernel_spmd(nc, maps, core_ids=list(range(NCORES)))
    out = np.concatenate([np.asarray(r["y"]) for r in res.results], axis=0)
    return out.astype(np.float32)
```

```python
from contextlib import ExitStack
import numpy as np
import concourse.bass as bass
import concourse.mybir as mybir
from concourse.bass_utils import run_bass_kernel_spmd

F32 = mybir.dt.float32
BF16 = mybir.dt.bfloat16
U32 = mybir.dt.uint32
ALU = mybir.AluOpType
AF = mybir.ActivationFunctionType
AX = mybir.AxisListType

NCORES = 8
NB = 2
SEQ = 2048
DM = 1024
MEM = 256
INW = 8776
NEG = -30000.0
NEGF = -1.0e30
NITER = 16
OFF = dict(mq=0, mk=512, mv=1024, mg=1536, dq=2048, dk=2560, dv=3072, dg=3584,
           iq=4096, ik=4608, iw=4672, xq=4680, xg=5192, gl=5704)
C_ID, C_R, C_CBT, C_CBQ, C_PW, C_END = 0, 128, 256, 768, 1280, 1280 + NITER


class TT:
    def __init__(self, h, name):
        self.h = h
        self.name = name
        self.w = None
        self.r = {}
        self.dsem = None
        self.dcnt = 0
        self.osem = None
        self.ocnt = 0
        self.psum = False

    def view(self, tag):
        return TT(self.h, self.name + "_" + str(tag))


class Sync:
    def __init__(self, nc):
        self.nc = nc
        self.eng = {"pe": nc.tensor, "act": nc.scalar, "dve": nc.vector, "pool": nc.gpsimd, "sp": nc.sync}
        self.sem = {k: nc.alloc_semaphore("sem_" + k) for k in ("pe", "act", "dve", "pool")}
        self.cnt = {k: 0 for k in self.sem}
        self.seen = {k: {} for k in self.eng}
        self.dsems = []
        self.osems = []
        self.nsb = 0
        self.outs = []
        self.dead = False

    def tile(self, stack, name, shape, dtype, psum=False):
        self.nsb += 1
        nm = "%s_%d" % (name, self.nsb)
        if psum:
            h = stack.enter_context(self.nc.psum_tensor(nm, shape, dtype))
        else:
            h = stack.enter_context(self.nc.sbuf_tensor(nm, shape, dtype))
        t = TT(h, nm)
        t.psum = psum
        return t

    def _wait(self, E, deps):
        best = {}
        for d in deps:
            if d is None:
                continue
            key, h, v, src = d
            if src == E and E == "pe":
                continue
            if self.seen[E].get(key, 0) >= v:
                continue
            if key not in best or best[key][1] < v:
                best[key] = (h, v)
        for key, (h, v) in best.items():
            self.eng[E].wait_ge(h, v)
            self.seen[E][key] = v

    def _deps(self, r, w):
        deps = []
        for t in r:
            deps.append(t.w)
        for t in w:
            deps.append(t.w)
            deps.extend(t.r.values())
        return deps

    def _record(self, me, r, w):
        for t in w:
            t.w = me
            t.r = {}
        for t in r:
            if t in w:
                continue
            old = t.r.get(me[0])
            if old is None or old[2] < me[2]:
                t.r[me[0]] = me

    def op(self, E, fn, r=(), w=()):
        if self.dead:
            return None
        w = list(w) + [t for t in r if t.psum and t not in w]
        self._wait(E, self._deps(r, w))
        inst = fn(self.eng[E])
        self.cnt[E] += 1
        inst.then_inc(self.sem[E], 1)
        me = (E, self.sem[E], self.cnt[E], E)
        self._record(me, r, w)
        return inst

    def dma(self, E, out, in_, t, load=True, **kw):
        if self.dead:
            return None
        if load:
            self._wait(E, self._deps((), (t,)))
        else:
            self._wait(E, self._deps((t,), ()))
        if load:
            if t.dsem is None:
                t.dsem = self.nc.alloc_semaphore("dsem_%d" % len(self.dsems))
                self.dsems.append(t)
            inst = self.eng[E].dma_start(out=out, in_=in_, **kw)
            t.dcnt += 16
            inst.then_inc(t.dsem, 16)
            me = ("d_" + t.name, t.dsem, t.dcnt, None)
            self._record(me, (), (t,))
        else:
            if t.osem is None:
                t.osem = self.nc.alloc_semaphore("osem_%d" % len(self.osems))
                self.osems.append(t)
            inst = self.eng[E].dma_start(out=out, in_=in_, **kw)
            t.ocnt += 16
            inst.then_inc(t.osem, 16)
            me = ("o_" + t.name, t.osem, t.ocnt, None)
            self._record(me, (t,), ())
            self.outs.append(me)
        return inst

    def barrier(self):
        if self.dead:
            return
        deps = [(k, self.sem[k], self.cnt[k], k) for k in self.sem if self.cnt[k] > 0]
        for t in self.dsems:
            if t.dcnt > 0:
                deps.append(("d_" + t.name, t.dsem, t.dcnt, None))
        for t in self.osems:
            if t.ocnt > 0:
                deps.append(("o_" + t.name, t.osem, t.ocnt, None))
        for E in self.eng:
            self._wait(E, deps)

    def finish(self):
        self._wait("sp", self.outs)
        self.barrier()


class StopBuild(Exception):
    pass


def build(dbg=None, nb_run=NB, stop=None):
    nc = bass.Bass("TRN2", target_bir_lowering=False)
    S = Sync(nc)
    x = nc.dram_tensor("x", [NB, SEQ, DM], F32, kind="ExternalInput").ap()
    mem = nc.dram_tensor("mem", [NB, MEM, DM], F32, kind="ExternalInput").ap()
    w_in = nc.dram_tensor("w_in", [DM, INW], F32, kind="ExternalInput").ap()
    w_kv = nc.dram_tensor("w_kv", [DM, 1024], F32, kind="ExternalInput").ap()
    w_up = nc.dram_tensor("w_up", [3, 512, DM], F32, kind="ExternalInput").ap()
    w_out = nc.dram_tensor("w_out", [DM, DM], F32, kind="ExternalInput").ap()
    vecs = nc.dram_tensor("vecs", [128, 40], F32, kind="ExternalInput").ap()
    gfin = nc.dram_tensor("gfin", [128, DM], F32, kind="ExternalInput").ap()
    ropet = nc.dram_tensor("ropet", [128, 2 * SEQ], F32, kind="ExternalInput").ap()
    consts = nc.dram_tensor("consts", [128, C_END], F32, kind="ExternalInput").ap()
    y = nc.dram_tensor("y", [NB, SEQ, DM], F32, kind="ExternalOutput").ap()
    dbg_out = {}
    if dbg:
        for k, shp in dbg.items():
            dbg_out[k] = nc.dram_tensor("dbg_" + k, list(shp), F32, kind="ExternalOutput").ap()

    op = S.op
    root = ExitStack()
    with root:
        pA = S.tile(root, "pA", [128, 512], F32, psum=True)
        pB = S.tile(root, "pB", [128, 512], F32, psum=True)
        pR = S.tile(root, "pR", [128, 512], F32, psum=True)
        pS = [S.tile(root, "pS%d" % i, [128, 512], F32, psum=True) for i in range(2)]
        pV = [S.tile(root, "pV%d" % i, [128, 512], F32, psum=True) for i in range(2)]
        pT = S.tile(root, "pT", [128, 8, 128], BF16, psum=True)
        pAB = [pA, pB]
        cosT = S.tile(root, "cosT", [128, SEQ], F32)
        sinT = S.tile(root, "sinT", [128, SEQ], F32)
        cst = S.tile(root, "cst", [128, C_END - C_CBQ], F32)
        identb = S.tile(root, "identb", [128, 128], BF16)
        Rb = S.tile(root, "Rb", [128, 128], BF16)
        cbTb = S.tile(root, "cbTb", [128, 2, 256], BF16)
        vec = S.tile(root, "vec", [128, 40], F32)
        epst = S.tile(root, "epst", [128, 1], F32)
        wst = [S.tile(root, "wst%d" % i, [128, 512], F32) for i in range(2)]
        hT = S.tile(root, "hT", [128, 8, SEQ], BF16)
        smallf = S.tile(root, "smallf", [128, 64], F32)
        ss_v = [smallf.view("ss%d" % i) for i in range(4)]
        rs_v = [smallf.view("rs%d" % i) for i in range(4)]

        S.dma("sp", cosT.h[:, :], ropet[:, 0:SEQ], cosT)
        S.dma("sp", sinT.h[:, :], ropet[:, SEQ:2 * SEQ], sinT)
        S.dma("sp", cst.h[:, :], consts[:, C_CBQ:C_END], cst)
        S.dma("sp", vec.h[:, :], vecs[:, :], vec)
        with ExitStack() as st_s:
            cst0 = S.tile(st_s, "cst0", [128, C_CBQ], F32)
            S.dma("sp", cst0.h[:, :], consts[:, 0:C_CBQ], cst0)
            op("dve", lambda e: e.tensor_copy(identb.h[:, :], cst0.h[:, C_ID:C_ID + 128]), r=[cst0], w=[identb])
            op("dve", lambda e: e.tensor_copy(Rb.h[:, :], cst0.h[:, C_R:C_R + 128]), r=[cst0], w=[Rb])
            op("dve", lambda e: e.tensor_copy(cbTb.h[:, :, :], cst0.h[:, C_CBT:C_CBT + 512].rearrange("p (a b) -> p a b", a=2)),
               r=[cst0], w=[cbTb])
            S.barrier()
        op("dve", lambda e: e.memset(epst.h[:, :], 1e-6), w=[epst])
        wst_i = [0]
        scr_i = [0]
        xi_i = [0]
        xst_i = [0]

        def load_w(dst, ncols, src_fn, stg=None):
            nchunk = dst.h.shape[1]
            if stg is None:
                stg = wst
            for c in range(nchunk):
                for b0 in range(0, ncols, 512):
                    bw = min(512, ncols - b0)
                    st = stg[wst_i[0] % len(stg)]
                    wst_i[0] += 1
                    S.dma("sp", st.h[:, 0:bw], src_fn(c)[:, b0:b0 + bw], st)
                    if wst_i[0] % 2 == 0:
                        op("act", lambda e: e.activation(out=dst.h[:, c, b0:b0 + bw], in_=st.h[:, 0:bw], func=AF.Copy), r=[st], w=[dst])
                    else:
                        op("dve", lambda e: e.tensor_copy(dst.h[:, c, b0:b0 + bw], st.h[:, 0:bw]), r=[st], w=[dst])

        def win_src(col0, ncols):
            return lambda c: w_in[c * 128:(c + 1) * 128, col0:col0 + ncols]

        def rmsnorm_T(src_fn, ntiles, dstT, gcol, xst, xn):
            for tt in range(ntiles):
                xs = xst[xst_i[0] % len(xst)]
                xnn = xn[xst_i[0] % len(xn)]
                xst_i[0] += 1
                k = tt % 4
                ssv, rsv = ss_v[k], rs_v[k]
                S.dma("sp", xs.h[:, :], src_fn(tt), xs)
                op("act", lambda e: e.activation(out=xnn.h[:, :], in_=xs.h[:, :], func=AF.Square,
                                                 accum_out=smallf.h[:, k:k + 1]), r=[xs], w=[xnn, ssv])
                op("act", lambda e: e.activation(out=smallf.h[:, 4 + k:5 + k], in_=smallf.h[:, k:k + 1], func=AF.Sqrt,
                                                 scale=1.0 / DM, bias=epst.h[:, 0:1]), r=[ssv, epst], w=[rsv])
                op("dve", lambda e: e.reciprocal(smallf.h[:, 4 + k:5 + k], smallf.h[:, 4 + k:5 + k]), r=[rsv], w=[rsv])
                op("dve", lambda e: e.tensor_scalar(out=xnn.h[:, :], in0=xs.h[:, :], scalar1=smallf.h[:, 4 + k:5 + k], scalar2=None,
                                                    op0=ALU.mult), r=[xs, rsv], w=[xnn])
                for c in range(8):
                    op("pe", lambda e: e.transpose(pT.h[:, c, :], xnn.h[:, c * 128:(c + 1) * 128], identb.h[:, :]),
                       r=[xnn, identb], w=[pT])
                op("dve", lambda e: e.tensor_tensor(out=dstT.h[:, :, tt * 128:(tt + 1) * 128], in0=pT.h[:, :, :],
                                                    in1=vec.h[:, gcol:gcol + 8].unsqueeze(2).to_broadcast([128, 8, 128]), op=ALU.mult),
                   r=[pT, vec], w=[dstT])

        ab_i = [0]

        def proj_fm(W, col0, rhsT, t0, n, m=128, ps=None):
            if ps is None:
                ps = pAB[ab_i[0] % 2]
                ab_i[0] += 1
            nchunk = W.h.shape[1]
            for c in range(nchunk):
                op("pe", lambda e: e.matmul(ps.h[0:m, 0:n], lhsT=W.h[:, c, col0:col0 + m], rhs=rhsT.h[:, c, t0:t0 + n],
                                            start=(c == 0), stop=(c == nchunk - 1)), r=[W, rhsT], w=[ps])
            return ps

        def proj_tm(W, col0, ncols, lhsT_, t0, ps=None):
            if ps is None:
                ps = pAB[ab_i[0] % 2]
                ab_i[0] += 1
            nchunk = W.h.shape[1]
            for c in range(nchunk):
                op("pe", lambda e: e.matmul(ps.h[:, 0:ncols], lhsT=lhsT_.h[:, c, t0:t0 + 128], rhs=W.h[:, c, col0:col0 + ncols],
                                            start=(c == 0), stop=(c == nchunk - 1)), r=[W, lhsT_], w=[ps])
            return ps

        def rope(ps, n, pos0, dst_ap, dst, scr):
            if isinstance(scr, list):
                scr_i[0] += 1
                scr = scr[scr_i[0] % len(scr)]
            zb, t1, t2 = scr
            ck("kp")
            op("act", lambda e: e.activation(out=zb.h[:, 0:n], in_=ps.h[:, 0:n], func=AF.Copy), r=[ps], w=[zb])
            ck("kr0")
            op("pe", lambda e: e.matmul(pR.h[:, 0:n], lhsT=Rb.h[:, :], rhs=zb.h[:, 0:n], start=True, stop=True),
               r=[Rb, zb], w=[pR])
            ck("kr1")
            op("dve", lambda e: e.tensor_tensor(out=t1.h[:, 0:n], in0=ps.h[:, 0:n], in1=cosT.h[:, pos0:pos0 + n], op=ALU.mult),
               r=[ps, cosT], w=[t1])
            op("dve", lambda e: e.tensor_tensor(out=t2.h[:, 0:n], in0=pR.h[:, 0:n], in1=sinT.h[:, pos0:pos0 + n], op=ALU.mult),
               r=[pR, sinT], w=[t2])
            ck("kr2")
            if isinstance(dst_ap, tuple):
                op("pool", lambda e: e.tensor_tensor(out=dst_ap[0], in0=t1.h[0:64, 0:n], in1=t2.h[0:64, 0:n], op=ALU.add),
                   r=[t1, t2], w=[dst])
                op("pool", lambda e: e.tensor_tensor(out=dst_ap[1], in0=t1.h[64:128, 0:n], in1=t2.h[64:128, 0:n], op=ALU.add),
                   r=[t1, t2], w=[dst])
            else:
                op("pool", lambda e: e.tensor_tensor(out=dst_ap, in0=t1.h[:, 0:n], in1=t2.h[:, 0:n], op=ALU.add),
                   r=[t1, t2], w=[dst])

        def dump(name, t, ap, shape):
            if name not in dbg_out:
                return
            with ExitStack() as es:
                tmp = S.tile(es, "dbgtmp", [128, 512], F32)
                if len(shape) == 2:
                    pieces = [(ap[:, c0:min(c0 + 512, shape[1])], dbg_out[name][:, c0:min(c0 + 512, shape[1])], min(512, shape[1] - c0))
                              for c0 in range(0, shape[1], 512)]
                else:
                    pieces = [(ap[:, a, c0:min(c0 + 512, shape[2])], dbg_out[name][:, a, c0:min(c0 + 512, shape[2])], min(512, shape[2] - c0))
                              for a in range(shape[1]) for c0 in range(0, shape[2], 512)]
                for src, dst, n in pieces:
                    op("dve", lambda e: e.tensor_copy(tmp.h[:, 0:n], src), r=[t], w=[tmp])
                    S.dma("sp", dst, tmp.h[:, 0:n], tmp, load=False)
                S.barrier()

        def attn_out(acc_fn, sg, abt, aT, tok0, nheads, hd):
            pass

        def ck(name):
            if stop == name:
                S.barrier()
                S.dead = True

        try:
          for b in range(nb_run):
              with ExitStack() as st0:
                  xst = [S.tile(st0, "xst%d" % i, [128, 1024], F32) for i in range(4)]
                  xn = [S.tile(st0, "xn%d" % i, [128, 1024], BF16) for i in range(4)]
                  rmsnorm_T(lambda tt: x[b, tt * 128:(tt + 1) * 128, :], SEQ // 128, hT, 0, xst, xn)
                  S.barrier()
              if b == 0:
                  dump("hT", hT, hT.h[:, :, :], (128, 8, SEQ))
              ck("stage0")

              with ExitStack() as st_a:
                  aT_dsa = S.tile(st_a, "aT_dsa", [128, 4, SEQ], BF16)
                  with ExitStack() as st:
                      KT = S.tile(st, "KT", [128, 4, SEQ], BF16)
                      VA = S.tile(st, "VA", [128, 16, 8, 65], BF16)
                      wA = S.tile(st, "wA", [128, 8, 512], BF16)
                      wB = S.tile(st, "wB", [128, 8, 512], BF16)
                      wC = S.tile(st, "wC", [128, 8, 512], BF16)
                      wI = S.tile(st, "wI", [128, 8, 8], BF16)
                      ikT = S.tile(st, "ikT", [128, SEQ], BF16)
                      stk = ExitStack()
                      zb = S.tile(stk, "zb", [128, 512], BF16)
                      t1 = S.tile(stk, "t1", [128, 512], F32)
                      t2 = S.tile(stk, "t2", [128, 512], F32)
                      scr = (zb, t1, t2)
                      wst6 = wst + [S.tile(stk, "wstk%d" % i, [128, 512], F32) for i in range(4)]
                      load_w(wA, 512, win_src(OFF["dk"], 512), stg=wst6)
                      load_w(wB, 512, win_src(OFF["dv"], 512), stg=wst6)
                      load_w(wC, 64, win_src(OFF["ik"], 64), stg=wst6)
                      ck("kw")
                      op("pool", lambda e: e.tensor_copy(wC.h[:, :, 64:128], wC.h[:, :, 0:64]), r=[wC], w=[wC])
                      op("pool", lambda e: e.memset(VA.h[:, :, :, 64:65], 1.0), w=[VA])
                      ck("kw2")
                      for tg in range(4):
                          for pr in range(4):
                              ps = proj_fm(wA, pr * 128, hT, tg * 512, 512)
                              rope(ps, 512, tg * 512, KT.h[:, pr, tg * 512:(tg + 1) * 512], KT, scr)
                              ck("kr")
                          ps = proj_fm(wC, 0, hT, tg * 512, 512)
                          rope(ps, 512, tg * 512, ikT.h[:, tg * 512:(tg + 1) * 512], ikT, scr)
                          for t4 in range(4):
                              tt = tg * 4 + t4
                              ps = proj_tm(wB, 0, 512, hT, tt * 128)
                              op("act", lambda e: e.activation(out=VA.h[:, tt, :, 0:64],
                                                               in_=ps.h[:, :].rearrange("p (h d) -> p h d", h=8),
                                                               func=AF.Copy), r=[ps], w=[VA])
                      if b == 0:
                          dump("dKT", KT, KT.h[:, :, :], (128, 4, SEQ))
                          dump("ikT", ikT, ikT.h[:, :], (128, SEQ))
                      ck("kside")
                      S.barrier()
                      stk.close()
                      sgall = S.tile(st, "sgall", [128, 16, 512], BF16)
                      with ExitStack() as stq:
                          wst6 = wst + [S.tile(stq, "wstq%d" % i, [128, 512], F32) for i in range(4)]
                          load_w(wB, 512, win_src(OFF["dg"], 512), stg=wst6)
                          load_w(wA, 512, win_src(OFF["dq"], 512), stg=wst6)
                          load_w(wC, 512, win_src(OFF["iq"], 512), stg=wst6)
                          load_w(wI, 8, win_src(OFF["iw"], 8), stg=wst6)
                          for tt in range(16):
                              ps = proj_tm(wB, 0, 512, hT, tt * 128)
                              op("act", lambda e: e.activation(out=sgall.h[:, tt, :], in_=ps.h[:, :], func=AF.Silu), r=[ps], w=[sgall])
                          S.barrier()
                      zb = S.tile(st, "zbq", [128, 256], BF16)
                      t1 = S.tile(st, "t1q", [128, 256], F32)
                      t2 = S.tile(st, "t2q", [128, 256], F32)
                      scr = (zb, t1, t2)
                      accS = S.tile(st, "accS", [128, 8, 2, 65], F32)
                      rdv = S.tile(st, "rdv", [128, 8, 2], F32)
                      QT = [S.tile(st, "QT%d" % i, [128, 4, 2, 256], BF16) for i in range(2)]
                      for i in range(2):
                          op("pool", lambda e: e.memset(QT[i].h[:, :, :, :], 0.0), w=[QT[i]])
                      iqT = S.tile(st, "iqT", [128, 4, 256], BF16)
                      abt = [S.tile(st, "abt%d" % i, [128, 512], BF16) for i in range(2)]
                      rl = [S.tile(st, "rl%d" % i, [128, 512], F32) for i in range(2)]
                      sc = [S.tile(st, "sc%d" % i, [128, SEQ], F32) for i in range(2)]
                      mb = [[S.tile(st, "mb%d_%d" % (j, i), [128, SEQ], BF16) for i in range(2)] for j in range(2)]
                      cnts = S.tile(st, "cnts", [128, 4], F32)
                      cnt0 = cnts.view("c0")
                      cnt1 = cnts.view("c1")
                      cthr = cnts.view("thr")
                      PT = [S.tile(st, "PT%d" % i, [128, 2, 256], BF16) for i in range(2)]
                      iws = S.tile(st, "iws", [128, 2, 8], F32)
                      bis = S.tile(st, "bis", [128, 16], F32)
                      wk_all = S.tile(st, "wk_all", [128, 2, NITER], F32)
                      gem = S.tile(st, "gem", [128, 2], U32)
                      rd = S.tile(st, "rd", [128, 4], F32)

                      op("dve", lambda e: e.memset(bis.h[:, :], 0.0), w=[bis])

                      def X_chunks(m):
                          N = 256 * (m + 1)
                          tok0 = 256 * m
                          bi = m % 2
                          ch = []

                          def c_proj(pr):
                              ps = proj_fm(wA, pr * 128, hT, tok0, 256)
                              rope(ps, 256, tok0, (QT[bi].h[0:64, pr, 0, :], QT[bi].h[64:128, pr, 1, :]), QT[bi], scr)
                              ps = proj_fm(wC, pr * 128, hT, tok0, 256)
                              rope(ps, 256, tok0, iqT.h[:, pr, :], iqT, scr)
                          for pr in range(4):
                              ch.append(lambda pr=pr: c_proj(pr))

                          def c_gate(t):
                              ps = proj_tm(wI, 0, 8, hT, tok0 + t * 128)
                              op("dve", lambda e: e.tensor_scalar(out=iws.h[:, t, :], in0=ps.h[:, 0:8], scalar1=0.125 * (8 ** -0.5),
                                                                  scalar2=None, op0=ALU.mult), r=[ps], w=[iws])
                          ch.append(lambda: (c_gate(0), c_gate(1)))

                          def c_idx(t, k0):
                              kw = min(512, N - k0)
                              for g in range(8):
                                  p0 = (g % 2) * 64
                                  pss = [pA, pB, pR][xi_i[0] % 3]
                                  rlt = rl[xi_i[0] % 2]
                                  xi_i[0] += 1
                                  op("pe", lambda e: e.matmul(pss.h[:, 0:kw], lhsT=iqT.h[p0:p0 + 64, g // 2, t * 128:(t + 1) * 128],
                                                              rhs=ikT.h[p0:p0 + 64, k0:k0 + kw], start=True, stop=True),
                                     r=[iqT, ikT], w=[pss])
                                  op("act", lambda e: e.activation(out=rlt.h[:, 0:kw], in_=pss.h[:, 0:kw], func=AF.Relu),
                                     r=[pss], w=[rlt])
                                  if g == 0:
                                      op("dve", lambda e: e.tensor_scalar(out=sc[t].h[:, k0:k0 + kw], in0=rlt.h[:, 0:kw],
                                                                          scalar1=iws.h[:, t, 0:1], scalar2=None, op0=ALU.mult),
                                         r=[rlt, iws], w=[sc[t]])
                                  else:
                                      op("dve", lambda e: e.scalar_tensor_tensor(out=sc[t].h[:, k0:k0 + kw], in0=rlt.h[:, 0:kw],
                                                                                 scalar=iws.h[:, t, g:g + 1], in1=sc[t].h[:, k0:k0 + kw],
                                                                                 op0=ALU.mult, op1=ALU.add),
                                         r=[rlt, iws, sc[t]], w=[sc[t]])
                          for t in range(2):
                              for k0 in range(0, N, 512):
                                  ch.append(lambda t=t, k0=k0: c_idx(t, k0))

                          def c_pre():
                              if m == 0:
                                  op("dve", lambda e: e.memset(bis.h[:, 6:8], -1.0e29), w=[bis])
                              else:
                                  op("dve", lambda e: e.memset(cnts.h[:, 2:3], float(N - 256)), w=[cthr])
                                  op("dve", lambda e: e.memset(cnts.h[:, 3:4], float(N - 512)), w=[cthr])
                                  for t in range(2):
                                      op("dve", lambda e: e.tensor_reduce(out=bis.h[:, 10 + t:11 + t], in_=sc[t].h[:, 0:N], axis=AX.X, op=ALU.max,
                                                                          apply_absolute_value=True), r=[sc[t]], w=[bis])
                                  op("dve", lambda e: e.tensor_scalar(out=bis.h[:, 0:2], in0=bis.h[:, 10:12], scalar1=-1.0, scalar2=None, op0=ALU.mult),
                                     r=[bis], w=[bis])
                                  op("dve", lambda e: e.tensor_scalar(out=bis.h[:, 8:10], in0=bis.h[:, 10:12], scalar1=2.0, scalar2=None, op0=ALU.mult),
                                     r=[bis], w=[bis])
                                  op("dve", lambda e: e.tensor_tensor(out=wk_all.h[:, :, :],
                                                                      in0=bis.h[:, 8:10].unsqueeze(2).to_broadcast([128, 2, NITER]),
                                                                      in1=cst.h[:, C_PW - C_CBQ:C_PW - C_CBQ + NITER].unsqueeze(1).to_broadcast([128, 2, NITER]),
                                                                      op=ALU.mult), r=[bis, cst], w=[wk_all])
                              for t in range(2):
                                  op("dve", lambda e: e.tensor_tensor(out=sc[t].h[:, N - 256:N], in0=sc[t].h[:, N - 256:N],
                                                                      in1=cst.h[:, t * 256:(t + 1) * 256], op=ALU.add),
                                     r=[sc[t], cst], w=[sc[t]])
                          ch.append(c_pre)

                          def c_iter(it):
                              op("dve", lambda e: e.tensor_tensor(out=bis.h[:, 2:4], in0=bis.h[:, 0:2], in1=wk_all.h[:, :, it], op=ALU.add),
                                 r=[bis, wk_all], w=[bis])
                              op("act", lambda e: e.activation(out=mb[bi][1].h[:, 0:N], in_=sc[1].h[:, 0:N], func=AF.Sign, bias=bis.h[:, 3:4],
                                                               scale=-1.0, accum_out=cnts.h[:, 1:2]), r=[sc[1], bis], w=[mb[bi][1], cnt1])
                              op("dve", lambda e: e.tensor_scalar(out=mb[bi][0].h[:, 0:N], in0=sc[0].h[:, 0:N], scalar1=bis.h[:, 2:3],
                                                                  scalar2=None, op0=ALU.is_lt, op1=ALU.add,
                                                                  accum_out=cnts.h[:, 0:1]), r=[sc[0], bis], w=[mb[bi][0], cnt0])
                              op("dve", lambda e: e.tensor_tensor(out=gem.h[:, :], in0=cnts.h[:, 0:2], in1=cnts.h[:, 2:4], op=ALU.is_le),
                                 r=[cnt0, cnt1, cthr], w=[gem])
                              op("dve", lambda e: e.copy_predicated(bis.h[:, 0:2], gem.h[:, :], bis.h[:, 2:4]), r=[gem, bis], w=[bis])
                          if m > 0:
                              for it in range(NITER):
                                  ch.append(lambda it=it: c_iter(it))

                          def c_fin():
                              if m > 0:
                                  op("dve", lambda e: e.tensor_copy(bis.h[:, 6:8], bis.h[:, 0:2]), r=[bis], w=[bis])
                              for t in range(2):
                                  op("dve", lambda e: e.tensor_scalar(out=mb[bi][t].h[:, 0:N], in0=sc[t].h[:, 0:N], scalar1=bis.h[:, 6 + t:7 + t],
                                                                      scalar2=NEG, op0=ALU.is_lt, op1=ALU.mult), r=[sc[t], bis], w=[mb[bi][t]])
                          ch.append(c_fin)
                          return ch

                      for f in X_chunks(0):
                          f()
                      dsa_tail = [None]
                      for m in range(8):
                          N = 256 * (m + 1)
                          tok0 = 256 * m
                          bi = m % 2
                          nkt = N // 128
                          steps = [(h, jj) for h in range(8) for jj in range(nkt // 2)]
                          xc = X_chunks(m + 1) if m + 1 < 8 else []

                          def d_qk(i):
                              h, jj = steps[i]
                              p0 = (h % 2) * 64
                              pr = h // 2
                              pss = pS[i % 2]
                              for jl in range(2):
                                  j = jj * 2 + jl
                                  op("pe", lambda e: e.matmul(pss.h[:, jl * 256:(jl + 1) * 256], lhsT=KT.h[:, pr, j * 128:(j + 1) * 128],
                                                              rhs=QT[bi].h[:, pr, h % 2, :], start=True, stop=False), r=[KT, QT[bi]], w=[pss])
                                  for t in range(2):
                                      op("pe", lambda e: e.matmul(pss.h[:, jl * 256 + t * 128:jl * 256 + (t + 1) * 128],
                                                                  lhsT=mb[bi][t].h[:, j * 128:(j + 1) * 128], rhs=identb.h[:, :],
                                                                  start=False, stop=(t == 1)), r=[mb[bi][t], identb], w=[pss])

                          def d_rest(i):
                              h, jj = steps[i]
                              pss = pS[i % 2]
                              ptt = PT[i % 2]
                              acc = pV[h % 2]
                              op("act", lambda e: e.activation(out=ptt.h[:, :, :], in_=pss.h[:, :].rearrange("p (a b) -> p a b", a=2),
                                                               func=AF.Exp, scale=0.125), r=[pss], w=[ptt])
                              for jl in range(2):
                                  j = jj * 2 + jl
                                  for t in range(2):
                                      op("pe", lambda e: e.matmul(acc.h[:, t * 65:(t + 1) * 65], lhsT=ptt.h[:, jl, t * 128:(t + 1) * 128],
                                                                  rhs=VA.h[:, j, h, :], start=(j == 0 and t == 0), stop=(j == nkt - 1),
                                                                  skip_group_check=True),
                                         r=[ptt, VA], w=[acc])
                              if jj == nkt // 2 - 1:
                                  op("act", lambda e: e.activation(out=accS.h[:, h, :, :], in_=acc.h[:, 0:130].rearrange("p (t d) -> p t d", t=2),
                                                                   func=AF.Copy), r=[acc], w=[accS])

                          d_qk(0)
                          if dsa_tail[0] is not None:
                              dsa_tail[0]()
                          done = 0
                          for i in range(len(steps)):
                              if i + 1 < len(steps):
                                  d_qk(i + 1)
                              d_rest(i)
                              target = ((i + 1) * len(xc) + len(steps) - 1) // len(steps)
                              while done < target:
                                  xc[done]()
                                  done += 1
                          def make_dsa_tail(tok0=tok0):
                              def tail():
                                  op("dve", lambda e: e.reciprocal(rdv.h[:, :, :], accS.h[:, :, :, 64]), r=[accS], w=[rdv])
                                  for t in range(2):
                                      op("dve", lambda e: e.tensor_tensor(out=rl[t].h[:, :].rearrange("p (h d) -> p h d", h=8), in0=accS.h[:, :, t, 0:64],
                                                                          in1=rdv.h[:, :, t].unsqueeze(2).to_broadcast([128, 8, 64]), op=ALU.mult),
                                         r=[accS, rdv], w=[rl[t]])
                                      op("pool", lambda e: e.tensor_tensor(out=abt[t].h[:, :], in0=rl[t].h[:, :], in1=sgall.h[:, (tok0 // 128) + t, :], op=ALU.mult),
                                         r=[rl[t], sgall], w=[abt[t]])
                                      for c in range(4):
                                          op("pe", lambda e: e.transpose(pT.h[:, c, :], abt[t].h[:, c * 128:(c + 1) * 128], identb.h[:, :]),
                                             r=[abt[t], identb], w=[pT])
                                      op("dve", lambda e: e.tensor_copy(aT_dsa.h[:, :, tok0 + t * 128:tok0 + (t + 1) * 128], pT.h[:, 0:4, :]),
                                         r=[pT], w=[aT_dsa])
                              return tail
                          dsa_tail[0] = make_dsa_tail()
                      dsa_tail[0]()
                      S.barrier()
                  if b == 0:
                      dump("aT_dsa", aT_dsa, aT_dsa.h[:, :, :], (128, 4, SEQ))
                  S.barrier()
                  ck("dsa")
                  with ExitStack() as st_b:
                      aT_moba = S.tile(st_b, "aT_moba", [128, 4, SEQ], BF16)
                      with ExitStack() as st:
                          KT = S.tile(st, "KT", [128, 4, SEQ], BF16)
                          VA = S.tile(st, "VA", [128, 16, 8, 65], BF16)
                          wA = S.tile(st, "wA", [128, 8, 512], BF16)
                          wB = S.tile(st, "wB", [128, 8, 512], BF16)
                          scr = [(S.tile(st, "zb%d" % i, [128, 512], BF16), S.tile(st, "t1_%d" % i, [128, 512], F32),
                                  S.tile(st, "t2_%d" % i, [128, 512], F32)) for i in range(2)]
                          QT = [S.tile(st, "QT%d" % i, [128, 4, 2, 256], BF16) for i in range(2)]
                          QTg = S.tile(st, "QTg", [128, 4, 256], BF16)
                          for i in range(2):
                              op("pool", lambda e: e.memset(QT[i].h[:, :, :, :], 0.0), w=[QT[i]])
                          sgall = S.tile(st, "sgall", [128, 16, 512], BF16)
                          yb = [S.tile(st, "yb%d" % i, [128, 512], F32) for i in range(2)]
                          abt = [S.tile(st, "abt%d" % i, [128, 512], BF16) for i in range(2)]
                          PT = [S.tile(st, "PT%d" % i, [128, 2, 256], BF16) for i in range(3)]
                          pS3 = [pS[0], pS[1], pB]
                          kmf = S.tile(st, "kmf", [128, 4, 8], F32)
                          kmT = S.tile(st, "kmT", [128, 4, 16], BF16)
                          gt = S.tile(st, "gt", [128, 2, 8, 8], F32)
                          cmpb = S.tile(st, "cmpb", [128, 16, 8, 8], F32)
                          rank = S.tile(st, "rank", [128, 16, 8], F32)
                          sel = [S.tile(st, "sel%d" % i, [128, 2, 8, 8], F32) for i in range(2)]
                          accs = [S.tile(st, "accs%d" % i, [128, 8, 65], F32) for i in range(2)]
                          rdv = S.tile(st, "rdv", [128, 2, 8], F32)
                          wst6 = wst + [S.tile(st, "wstm%d" % i, [128, 512], F32) for i in range(4)]
                          load_w(wA, 512, win_src(OFF["mk"], 512), stg=wst6)
                          load_w(wB, 512, win_src(OFF["mv"], 512), stg=wst6)
                          op("pool", lambda e: e.memset(VA.h[:, :, :, 64:65], 1.0), w=[VA])
                          for tg in range(4):
                              for pr in range(4):
                                  ps = proj_fm(wA, pr * 128, hT, tg * 512, 512)
                                  rope(ps, 512, tg * 512, KT.h[:, pr, tg * 512:(tg + 1) * 512], KT, scr)
                              for t4 in range(4):
                                  tt = tg * 4 + t4
                                  ps = proj_tm(wB, 0, 512, hT, tt * 128)
                                  op("act", lambda e: e.activation(out=VA.h[:, tt, :, 0:64],
                                                                   in_=ps.h[:, :].rearrange("p (h d) -> p h d", h=8),
                                                                   func=AF.Copy), r=[ps], w=[VA])
                          ck("mk")
                          for pr in range(4):
                              op("dve", lambda e: e.tensor_reduce(out=kmf.h[:, pr, :], in_=KT.h[:, pr, :].rearrange("p (n k) -> p n k", n=8),
                                                                  axis=AX.X, op=ALU.add), r=[KT], w=[kmf])
                          op("dve", lambda e: e.memset(kmT.h[:, :, :], 0.0), w=[kmT])
                          op("dve", lambda e: e.tensor_scalar(out=kmT.h[0:64, :, 0:8], in0=kmf.h[0:64, :, :], scalar1=1.0 / 256.0, scalar2=None,
                                                              op0=ALU.mult), r=[kmf], w=[kmT])
                          op("dve", lambda e: e.tensor_scalar(out=kmT.h[64:128, :, 8:16], in0=kmf.h[64:128, :, :], scalar1=1.0 / 256.0, scalar2=None,
                                                              op0=ALU.mult), r=[kmf], w=[kmT])
                          ck("mkm")
                          load_w(wB, 512, win_src(OFF["mg"], 512), stg=wst6)
                          load_w(wA, 512, win_src(OFF["mq"], 512), stg=wst6)
                          for tt in range(16):
                              ps = proj_tm(wB, 0, 512, hT, tt * 128)
                              op("act", lambda e: e.activation(out=sgall.h[:, tt, :], in_=ps.h[:, :], func=AF.Silu), r=[ps], w=[sgall])
                          def MX_chunks(m):
                              tok0 = 256 * m
                              bi = m % 2
                              ch = []

                              def c_proj(pr):
                                  ps = proj_fm(wA, pr * 128, hT, tok0, 256, ps=pA)
                                  rope(ps, 256, tok0, (QT[bi].h[0:64, pr, 0, :], QT[bi].h[64:128, pr, 1, :]), QT[bi], scr)
                              for pr in range(4):
                                  ch.append(lambda pr=pr: c_proj(pr))

                              def c_sel0():
                                  if m <= 3:
                                      op("dve", lambda e: e.memset(sel[bi].h[:, :, :, :], 1.0), w=[sel[bi]])
                                      return
                                  op("pool", lambda e: e.tensor_tensor(out=QTg.h[:, :, :], in0=QT[bi].h[:, :, 0, :], in1=QT[bi].h[:, :, 1, :], op=ALU.add),
                                     r=[QT[bi]], w=[QTg])
                                  for t in range(2):
                                      for pr in range(4):
                                          op("pe", lambda e: e.matmul(pR.h[:, t * 64 + pr * 16:t * 64 + pr * 16 + 16],
                                                                      lhsT=QTg.h[:, pr, t * 128:(t + 1) * 128],
                                                                      rhs=kmT.h[:, pr, :], start=True, stop=True),
                                             r=[QTg, kmT], w=[pR])
                                  op("dve", lambda e: e.tensor_copy(gt.h[:, :, :, :], pR.h[:, 0:128].rearrange("p (t h n) -> p t h n", t=2, h=8)),
                                     r=[pR], w=[gt])
                                  op("dve", lambda e: e.memset(gt.h[:, :, :, m:8], NEGF), w=[gt])
                              if m <= 3:
                                  ch.append(c_sel0)

                              def c_sel1():
                                  g3 = gt.h[:, :, :, :].rearrange("p t h n -> p (t h) n")
                                  op("dve", lambda e: e.tensor_tensor(out=cmpb.h[:, :, :, :], in0=g3.unsqueeze(2).to_broadcast([128, 16, 8, 8]),
                                                                      in1=g3.unsqueeze(3).to_broadcast([128, 16, 8, 8]), op=ALU.is_gt),
                                     r=[gt], w=[cmpb])
                                  op("dve", lambda e: e.tensor_reduce(out=rank.h[:, :, :], in_=cmpb.h[:, :, :, :], axis=AX.X, op=ALU.add),
                                     r=[cmpb], w=[rank])
                                  op("dve", lambda e: e.tensor_scalar(out=sel[bi].h[:, :, :, :].rearrange("p t h n -> p (t h) n"), in0=rank.h[:, :, :],
                                                                      scalar1=3.0, scalar2=None, op0=ALU.is_lt), r=[rank], w=[sel[bi]])
                                  op("dve", lambda e: e.memset(sel[bi].h[:, :, :, m:m + 1], 1.0), w=[sel[bi]])
                              if m > 3:
                                  ch.append(lambda: (c_sel0(), c_sel1()))
                              return ch

                          for f in MX_chunks(0):
                              f()
                          pending_tail = [None]
                          for m in range(8):
                              tok0 = 256 * m
                              bi = m % 2
                              xc = MX_chunks(m + 1) if m + 1 < 8 else []
                              steps = [(h, n) for h in range(8) for n in range(m + 1)]

                              def m_qk(i):
                                  h, n = steps[i]
                                  pr = h // 2
                                  pss = pS3[i % 3]
                                  for jl in range(2):
                                      j = 2 * n + jl
                                      op("pe", lambda e: e.matmul(pss.h[:, jl * 256:(jl + 1) * 256], lhsT=KT.h[:, pr, j * 128:(j + 1) * 128],
                                                                  rhs=QT[bi].h[:, pr, h % 2, :], start=True, stop=(n < m)), r=[KT, QT[bi]], w=[pss])
                                      if n == m:
                                          op("pe", lambda e: e.matmul(pss.h[:, jl * 256:(jl + 1) * 256], lhsT=identb.h[:, :], rhs=cbTb.h[:, jl, :],
                                                                      start=False, stop=True), r=[identb, cbTb], w=[pss])

                              def m_rest(i):
                                  h, n = steps[i]
                                  pss = pS3[i % 3]
                                  ptt = PT[i % 3]
                                  acc = pV[i % 2]
                                  op("act", lambda e: e.activation(out=ptt.h[:, :, :], in_=pss.h[:, :].rearrange("p (a b) -> p a b", a=2),
                                                                   func=AF.Exp, scale=0.125), r=[pss], w=[ptt])
                                  for jl in range(2):
                                      j = 2 * n + jl
                                      for t in range(2):
                                          op("pe", lambda e: e.matmul(acc.h[:, t * 65:(t + 1) * 65], lhsT=ptt.h[:, jl, t * 128:(t + 1) * 128],
                                                                      rhs=VA.h[:, j, h, :], start=(jl == 0 and t == 0), stop=(jl == 1),
                                                                      skip_group_check=True), r=[ptt, VA], w=[acc])
                                  for t in range(2):
                                      if n == 0:
                                          op("dve", lambda e: e.tensor_scalar(out=accs[t].h[:, h, :], in0=acc.h[:, t * 65:(t + 1) * 65],
                                                                              scalar1=sel[bi].h[:, t, h, 0:1], scalar2=None, op0=ALU.mult),
                                             r=[acc, sel[bi]], w=[accs[t]])
                                      else:
                                          op("dve", lambda e: e.scalar_tensor_tensor(out=accs[t].h[:, h, :], in0=acc.h[:, t * 65:(t + 1) * 65],
                                                                                     scalar=sel[bi].h[:, t, h, n:n + 1], in1=accs[t].h[:, h, :],
                                                                                     op0=ALU.mult, op1=ALU.add),
                                             r=[acc, sel[bi], accs[t]], w=[accs[t]])

                              m_qk(0)
                              if len(steps) > 1:
                                  m_qk(1)
                              if pending_tail[0] is not None:
                                  pending_tail[0]()
                              done = 0
                              for i in range(len(steps)):
                                  if i + 2 < len(steps):
                                      m_qk(i + 2)
                                  m_rest(i)
                                  target = ((i + 1) * len(xc) + len(steps) - 1) // len(steps)
                                  while done < target:
                                      xc[done]()
                                      done += 1
                              def make_tail(tok0=tok0, bi=bi):
                                  def tail():
                                      for t in range(2):
                                          op("dve", lambda e: e.reciprocal(rdv.h[:, t, :], accs[t].h[:, :, 64]), r=[accs[t]], w=[rdv])
                                          op("dve", lambda e: e.tensor_tensor(out=yb[t].h[:, :].rearrange("p (h d) -> p h d", h=8), in0=accs[t].h[:, :, 0:64],
                                                                              in1=rdv.h[:, t, :].unsqueeze(2).to_broadcast([128, 8, 64]), op=ALU.mult),
                                             r=[accs[t], rdv], w=[yb[t]])
                                          op("pool", lambda e: e.tensor_tensor(out=abt[t].h[:, :], in0=yb[t].h[:, :], in1=sgall.h[:, (tok0 // 128) + t, :], op=ALU.mult),
                                             r=[yb[t], sgall], w=[abt[t]])
                                          for c in range(4):
                                              op("pe", lambda e: e.transpose(pT.h[:, c, :], abt[t].h[:, c * 128:(c + 1) * 128], identb.h[:, :]),
                                                 r=[abt[t], identb], w=[pT])
                                          op("dve", lambda e: e.tensor_copy(aT_moba.h[:, :, tok0 + t * 128:tok0 + (t + 1) * 128], pT.h[:, 0:4, :]),
                                             r=[pT], w=[aT_moba])
                                  return tail
                              pending_tail[0] = make_tail()
                          pending_tail[0]()
                          S.barrier()
                      if b == 0:
                          dump("aT_moba", aT_moba, aT_moba.h[:, :, :], (128, 4, SEQ))
                      S.barrier()
                      ck("moba")
                      with ExitStack() as st_c:
                          aT_x = S.tile(st_c, "aT_x", [128, 4, SEQ], BF16)
                          with ExitStack() as st:
                              memT = S.tile(st, "memT", [128, 8, MEM], BF16)
                              wK = S.tile(st, "wK", [128, 8, 1024], BF16)
                              xkT = S.tile(st, "xkT", [128, 4, MEM], BF16)
                              xva = S.tile(st, "xva", [128, 2, 4, 129], BF16)
                              wA = S.tile(st, "wA", [128, 8, 512], BF16)
                              wB = S.tile(st, "wB", [128, 8, 512], BF16)
                              xqT2 = [S.tile(st, "xqT%d" % i, [128, 4, 256], BF16) for i in range(2)]
                              sg2 = [[S.tile(st, "sg%d_%d" % (j, i), [128, 512], F32) for i in range(2)] for j in range(2)]
                              yb2 = [[S.tile(st, "yb%d_%d" % (j, i), [128, 512], F32) for i in range(2)] for j in range(2)]
                              abt = [S.tile(st, "abt%d" % i, [128, 512], BF16) for i in range(2)]
                              PT = [S.tile(st, "PT%d" % i, [128, 2, 256], BF16) for i in range(2)]
                              rd = S.tile(st, "rd", [128, 4], F32)
                              xst = [S.tile(st, "xst%d" % i, [128, 1024], F32) for i in range(2)]
                              xn = [S.tile(st, "xn%d" % i, [128, 1024], BF16) for i in range(2)]
                              rmsnorm_T(lambda tt: mem[b, tt * 128:(tt + 1) * 128, :], MEM // 128, memT, 8, xst, xn)
                              wst6 = wst + [S.tile(st, "wstx%d" % i, [128, 512], F32) for i in range(4)]
                              load_w(wK, 1024, lambda c: w_kv[c * 128:(c + 1) * 128, :], stg=wst6)
                              op("pool", lambda e: e.memset(xva.h[:, :, :, 128:129], 1.0), w=[xva])
                              for h in range(4):
                                  ps = proj_fm(wK, h * 128, memT, 0, MEM)
                                  op("act", lambda e: e.activation(out=xkT.h[:, h, :], in_=ps.h[:, 0:MEM], func=AF.Copy), r=[ps], w=[xkT])
                              for mt in range(2):
                                  ps = proj_tm(wK, 512, 512, memT, mt * 128)
                                  op("act", lambda e: e.activation(out=xva.h[:, mt, :, 0:128], in_=ps.h[:, :].rearrange("p (h d) -> p h d", h=4),
                                                                   func=AF.Copy), r=[ps], w=[xva])
                              load_w(wA, 512, win_src(OFF["xq"], 512), stg=wst6)
                              load_w(wB, 512, win_src(OFF["xg"], 512), stg=wst6)
                              def x_proj(m):
                                  tok0 = 256 * m
                                  bi = m % 2
                                  for h in range(4):
                                      ps = proj_fm(wA, h * 128, hT, tok0, 256)
                                      op("act", lambda e: e.activation(out=xqT2[bi].h[:, h, :], in_=ps.h[:, 0:256], func=AF.Copy), r=[ps], w=[xqT2[bi]])
                                  for t in range(2):
                                      ps = proj_tm(wB, 0, 512, hT, tok0 + t * 128)
                                      op("act", lambda e: e.activation(out=sg2[bi][t].h[:, :], in_=ps.h[:, :], func=AF.Silu), r=[ps], w=[sg2[bi][t]])

                              def x_qk(m, h):
                                  bi = m % 2
                                  pss = pS[h % 2]
                                  for mt in range(2):
                                      op("pe", lambda e: e.matmul(pss.h[:, mt * 256:(mt + 1) * 256], lhsT=xkT.h[:, h, mt * 128:(mt + 1) * 128],
                                                                  rhs=xqT2[bi].h[:, h, :], start=True, stop=True), r=[xkT, xqT2[bi]], w=[pss])

                              def x_rest(m, h):
                                  bi = m % 2
                                  pss = pS[h % 2]
                                  ptt = PT[h % 2]
                                  acc = pV[h % 2]
                                  op("act", lambda e: e.activation(out=ptt.h[:, :, :], in_=pss.h[:, :].rearrange("p (a b) -> p a b", a=2),
                                                                   func=AF.Exp, scale=128.0 ** -0.5), r=[pss], w=[ptt])
                                  for mt in range(2):
                                      for t in range(2):
                                          op("pe", lambda e: e.matmul(acc.h[:, t * 129:(t + 1) * 129], lhsT=ptt.h[:, mt, t * 128:(t + 1) * 128],
                                                                      rhs=xva.h[:, mt, h, :], start=(mt == 0 and t == 0), stop=(mt == 1),
                                                                      skip_group_check=True), r=[ptt, xva], w=[acc])
                                  for t in range(2):
                                      op("dve", lambda e: e.reciprocal(rd.h[:, t:t + 1], acc.h[:, t * 129 + 128:t * 129 + 129]), r=[acc], w=[rd])
                                      op("dve", lambda e: e.tensor_scalar(out=yb2[bi][t].h[:, h * 128:(h + 1) * 128], in0=acc.h[:, t * 129:t * 129 + 128],
                                                                          scalar1=rd.h[:, t:t + 1], scalar2=None, op0=ALU.mult),
                                         r=[acc, rd], w=[yb2[bi][t]])

                              def x_tail(m):
                                  tok0 = 256 * m
                                  bi = m % 2
                                  for t in range(2):
                                      op("pool", lambda e: e.tensor_tensor(out=abt[t].h[:, :], in0=yb2[bi][t].h[:, :], in1=sg2[bi][t].h[:, :], op=ALU.mult),
                                         r=[yb2[bi][t], sg2[bi][t]], w=[abt[t]])
                                      for c in range(4):
                                          op("pe", lambda e: e.transpose(pT.h[:, c, :], abt[t].h[:, c * 128:(c + 1) * 128], identb.h[:, :]),
                                             r=[abt[t], identb], w=[pT])
                                      op("dve", lambda e: e.tensor_copy(aT_x.h[:, :, tok0 + t * 128:tok0 + (t + 1) * 128], pT.h[:, 0:4, :]),
                                         r=[pT], w=[aT_x])

                              x_proj(0)
                              for m in range(8):
                                  x_qk(m, 0)
                                  if m > 0:
                                      x_tail(m - 1)
                                  if m + 1 < 8:
                                      x_proj(m + 1)
                                  for h in range(4):
                                      if h + 1 < 4:
                                          x_qk(m, h + 1)
                                      x_rest(m, h)
                              x_tail(7)
                              S.barrier()
                          if b == 0:
                              dump("aT_x", aT_x, aT_x.h[:, :, :], (128, 4, SEQ))
                          S.barrier()
                          ck("cross")
                          with ExitStack() as st:
                              aTs = [aT_moba, aT_dsa, aT_x]
                              uT = S.tile(st, "uT", [128, 8, SEQ], BF16)
                              wO = S.tile(st, "wO", [128, 8, 1024], BF16)
                              gfb = S.tile(st, "gfb", [128, DM], F32)
                              stw = ExitStack()
                              w4 = [S.tile(stw, "w4st%d" % i, [128, 1024], F32) for i in range(2)]
                              wG = [S.tile(stw, "wG%d" % i, [128, 8, 384], BF16) for i in range(2)]
                              wU = [[S.tile(stw, "wU%d_%d" % (j, i), [128, 4, 128], BF16) for i in range(3)] for j in range(2)]
                              gsbs = [S.tile(stw, "gsb%d" % i, [128, 512], F32) for i in range(2)]
                              uaccs = [S.tile(stw, "uacc%d" % i, [128, 512], F32) for i in range(2)]
                              tmpus = [S.tile(stw, "tmpu%d" % i, [128, 512], F32) for i in range(2)]
                              pQ = [pA, pB, pS[0], pS[1]]
                              q_i = [0]
                              w4_i = [0]
                              S.dma("sp", gfb.h[:, :], gfin[:, :], gfb)

                              def load_fo(fo):
                                  bi = fo % 2
                                  for mi in range(3):
                                      stg = w4[w4_i[0] % 2]
                                      w4_i[0] += 1
                                      c0 = OFF["gl"] + mi * 1024 + fo * 128
                                      S.dma("sp", stg.h[:, :].rearrange("p (c n) -> p c n", c=8),
                                            w_in[:, c0:c0 + 128].rearrange("(c p) n -> p c n", p=128), stg)
                                      op("act", lambda e: e.activation(out=wG[bi].h[:, :, mi * 128:(mi + 1) * 128],
                                                                       in_=stg.h[:, :].rearrange("p (c n) -> p c n", c=8), func=AF.Copy),
                                         r=[stg], w=[wG[bi]])
                                      stg = w4[w4_i[0] % 2]
                                      w4_i[0] += 1
                                      S.dma("sp", stg.h[:, 0:512].rearrange("p (c n) -> p c n", c=4),
                                            w_up[mi, :, fo * 128:(fo + 1) * 128].rearrange("(c p) n -> p c n", p=128), stg)
                                      op("dve", lambda e: e.tensor_copy(wU[bi][mi].h[:, :, :],
                                                                        stg.h[:, 0:512].rearrange("p (c n) -> p c n", c=4)), r=[stg], w=[wU[bi][mi]])

                              load_fo(0)
                              for fo in range(8):
                                  bi = fo % 2
                                  if fo + 1 < 8:
                                      load_fo(fo + 1)
                                  else:
                                      for c in range(8):
                                          stg = w4[w4_i[0] % 2]
                                          w4_i[0] += 1
                                          S.dma("sp", stg.h[:, :], w_out[c * 128:(c + 1) * 128, :], stg)
                                          if c % 2 == 0:
                                              op("act", lambda e: e.activation(out=wO.h[:, c, :], in_=stg.h[:, :], func=AF.Copy), r=[stg], w=[wO])
                                          else:
                                              op("dve", lambda e: e.tensor_copy(wO.h[:, c, :], stg.h[:, :]), r=[stg], w=[wO])
                                  for tg in range(4):
                                      uacc = uaccs[tg % 2]
                                      for mi in range(3):
                                          gsb = gsbs[q_i[0] % 2]
                                          tmpu = tmpus[q_i[0] % 2]
                                          psg = proj_fm(wG[bi], mi * 128, hT, tg * 512, 512, ps=pQ[(2 * q_i[0]) % 4])
                                          op("act", lambda e: e.activation(out=gsb.h[:, :], in_=psg.h[:, :], func=AF.Sigmoid,
                                                                           bias=vec.h[:, 16 + mi * 8 + fo:17 + mi * 8 + fo]), r=[psg, vec], w=[gsb])
                                          psy = proj_fm(wU[bi][mi], 0, aTs[mi], tg * 512, 512, ps=pQ[(2 * q_i[0] + 1) % 4])
                                          q_i[0] += 1
                                          if mi == 0:
                                              op("dve", lambda e: e.tensor_tensor(out=uacc.h[:, :], in0=psy.h[:, :], in1=gsb.h[:, :], op=ALU.mult),
                                                 r=[psy, gsb], w=[uacc])
                                          else:
                                              op("dve", lambda e: e.tensor_tensor(out=tmpu.h[:, :], in0=psy.h[:, :], in1=gsb.h[:, :], op=ALU.mult),
                                                 r=[psy, gsb], w=[tmpu])
                                              if mi == 1:
                                                  op("pool", lambda e: e.tensor_tensor(out=uacc.h[:, :], in0=uacc.h[:, :], in1=tmpu.h[:, :], op=ALU.add),
                                                     r=[uacc, tmpu], w=[uacc])
                                              else:
                                                  op("pool", lambda e: e.tensor_tensor(out=uT.h[:, fo, tg * 512:(tg + 1) * 512], in0=uacc.h[:, :],
                                                                                       in1=tmpu.h[:, :], op=ALU.add), r=[uacc, tmpu], w=[uT])
                              S.barrier()
                              stw.close()
                              xst = [S.tile(st, "xst%d" % i, [128, 1024], F32) for i in range(4)]
                              xn = [S.tile(st, "xn%d" % i, [128, 1024], BF16) for i in range(2)]
                              for tt in range(SEQ // 128):
                                  xs = xst[xst_i[0] % len(xst)]
                                  xnn = xn[xst_i[0] % len(xn)]
                                  xst_i[0] += 1
                                  k = tt % 4
                                  ssv, rsv = ss_v[k], rs_v[k]
                                  S.dma("sp", xs.h[:, :], x[b, tt * 128:(tt + 1) * 128, :], xs)
                                  for half in range(2):
                                      ps = proj_tm(wO, half * 512, 512, uT, tt * 128)
                                      op("dve", lambda e: e.tensor_tensor(out=xs.h[:, half * 512:(half + 1) * 512], in0=ps.h[:, :],
                                                                          in1=xs.h[:, half * 512:(half + 1) * 512], op=ALU.add), r=[ps, xs], w=[xs])
                                  op("act", lambda e: e.activation(out=xnn.h[:, :], in_=xs.h[:, :], func=AF.Square,
                                                                   accum_out=smallf.h[:, k:k + 1]), r=[xs], w=[xnn, ssv])
                                  op("act", lambda e: e.activation(out=smallf.h[:, 4 + k:5 + k], in_=smallf.h[:, k:k + 1], func=AF.Sqrt,
                                                                   scale=1.0 / DM, bias=epst.h[:, 0:1]), r=[ssv, epst], w=[rsv])
                                  op("dve", lambda e: e.reciprocal(smallf.h[:, 4 + k:5 + k], smallf.h[:, 4 + k:5 + k]), r=[rsv], w=[rsv])
                                  op("dve", lambda e: e.scalar_tensor_tensor(out=xs.h[:, :], in0=xs.h[:, :], scalar=smallf.h[:, 4 + k:5 + k],
                                                                             in1=gfb.h[:, :], op0=ALU.mult, op1=ALU.mult),
                                     r=[xs, rsv, gfb], w=[xs])
                                  S.dma("pool", y[b, tt * 128:(tt + 1) * 128, :], xs.h[:, :], xs, load=False)
                              S.barrier()
        except StopBuild:
            pass
        S.dead = False
        S.finish()
    return nc


def rope_tables():
    half = 32
    inv = np.power(np.float32(10000.0), -np.arange(half, dtype=np.float32) * np.float32(2.0) / np.float32(64)).astype(np.float32)
    pos = np.arange(SEQ, dtype=np.float32)
    ang = (pos[:, None] * inv[None, :]).astype(np.float32)
    cos = np.cos(ang).astype(np.float32).T
    sin = np.sin(ang).astype(np.float32).T
    cosT = np.tile(cos, (4, 1))
    sinT = np.tile(sin, (4, 1))
    return np.ascontiguousarray(np.concatenate([cosT, sinT], axis=1))


def const_tables():
    c = np.zeros((128, C_END), np.float32)
    c[:, C_ID:C_ID + 128] = np.eye(128, dtype=np.float32)
    R = np.zeros((128, 128), np.float32)
    for hb in (0, 64):
        for d in range(32):
            R[hb + d + 32, hb + d] = -1.0
            R[hb + d, hb + d + 32] = 1.0
    c[:, C_R:C_R + 128] = R
    k = np.arange(128)[:, None]
    q = np.arange(256)[None, :]
    c[:, C_CBT:C_CBT + 256] = np.where(k <= q, 0.0, NEG)
    c[:, C_CBT + 256:C_CBT + 512] = np.where(k + 128 <= q, 0.0, NEG)
    qq = np.arange(128)[:, None]
    kk = np.arange(256)[None, :]
    c[:, C_CBQ:C_CBQ + 256] = np.where(kk <= qq, 0.0, NEGF)
    c[:, C_CBQ + 256:C_CBQ + 512] = np.where(kk <= qq + 128, 0.0, NEGF)
    c[:, C_PW:C_PW + NITER] = (0.5 ** np.arange(1, NITER + 1, dtype=np.float64)).astype(np.float32)[None, :]
    return c


def make_in_maps(x, mem, g_in, w_in, b_merge, g_mem, w_mem_kv, w_up_moba, w_up_dsa, w_up_cross, w_out, g_final):
    f = lambda a: np.ascontiguousarray(np.asarray(a, dtype=np.float32))
    x, mem = f(x), f(mem)
    vecs = np.zeros((128, 40), np.float32)
    vecs[:, 0:8] = f(g_in)[0].reshape(8, 128).T
    vecs[:, 8:16] = f(g_mem)[0].reshape(8, 128).T
    vecs[:, 16:40] = f(b_merge)[0].reshape(24, 128).T
    gfin = np.ascontiguousarray(np.broadcast_to(f(g_final)[None, :], (128, DM)))
    wup = np.ascontiguousarray(np.stack([f(w_up_moba)[0], f(w_up_dsa)[0], f(w_up_cross)[0]], axis=0))
    shared = dict(w_in=f(w_in)[0], w_kv=f(w_mem_kv)[0], w_up=wup, w_out=f(w_out)[0], vecs=vecs, gfin=gfin,
                  ropet=rope_tables(), consts=const_tables())
    maps = []
    for c in range(NCORES):
        d = dict(shared)
        d["x"] = np.ascontiguousarray(x[c * NB:(c + 1) * NB])
        d["mem"] = np.ascontiguousarray(mem[c * NB:(c + 1) * NB])
        maps.append(d)
    return maps


def kernel(**inputs):
    nc = build()
    maps = make_in_maps(**inputs)
    res = run_bass_kernel_spmd(nc, maps, core_ids=list(range(NCORES)))
    out = np.concatenate([np.asarray(r["y"]) for r in res.results], axis=0)
    return out.astype(np.float32)
```

```python
from contextlib import ExitStack
import numpy as np
import concourse.bass as bass
import concourse.mybir as mybir
from concourse.bass_utils import run_bass_kernel_spmd

F32 = mybir.dt.float32
BF16 = mybir.dt.bfloat16
U32 = mybir.dt.uint32
ALU = mybir.AluOpType
AF = mybir.ActivationFunctionType
AX = mybir.AxisListType

NCORES = 8
NB = 2
SEQ = 2048
DM = 1024
MEM = 256
INW = 8776
NEG = -30000.0
NEGF = -1.0e30
NITER = 16
OFF = dict(mq=0, mk=512, mv=1024, mg=1536, dq=2048, dk=2560, dv=3072, dg=3584,
           iq=4096, ik=4608, iw=4672, xq=4680, xg=5192, gl=5704)
C_ID, C_R, C_CBT, C_CBQ, C_PW, C_END = 0, 128, 256, 768, 1280, 1280 + NITER


class TT:
    def __init__(self, h, name):
        self.h = h
        self.name = name
        self.w = None
        self.r = {}
        self.dsem = None
        self.dcnt = 0
        self.osem = None
        self.ocnt = 0
        self.psum = False

    def view(self, tag):
        return TT(self.h, self.name + "_" + str(tag))


class Sync:
    def __init__(self, nc):
        self.nc = nc
        self.eng = {"pe": nc.tensor, "act": nc.scalar, "dve": nc.vector, "pool": nc.gpsimd, "sp": nc.sync}
        self.sem = {k: nc.alloc_semaphore("sem_" + k) for k in ("pe", "act", "dve", "pool")}
        self.cnt = {k: 0 for k in self.sem}
        self.seen = {k: {} for k in self.eng}
        self.dsems = []
        self.osems = []
        self.nsb = 0
        self.outs = []
        self.dead = False

    def tile(self, stack, name, shape, dtype, psum=False):
        self.nsb += 1
        nm = "%s_%d" % (name, self.nsb)
        if psum:
            h = stack.enter_context(self.nc.psum_tensor(nm, shape, dtype))
        else:
            h = stack.enter_context(self.nc.sbuf_tensor(nm, shape, dtype))
        t = TT(h, nm)
        t.psum = psum
        return t

    def _wait(self, E, deps):
        best = {}
        for d in deps:
            if d is None:
                continue
            key, h, v, src = d
            if src == E and E == "pe":
                continue
            if self.seen[E].get(key, 0) >= v:
                continue
            if key not in best or best[key][1] < v:
                best[key] = (h, v)
        for key, (h, v) in best.items():
            self.eng[E].wait_ge(h, v)
            self.seen[E][key] = v

    def _deps(self, r, w):
        deps = []
        for t in r:
            deps.append(t.w)
        for t in w:
            deps.append(t.w)
            deps.extend(t.r.values())
        return deps

    def _record(self, me, r, w):
        for t in w:
            t.w = me
            t.r = {}
        for t in r:
            if t in w:
                continue
            old = t.r.get(me[0])
            if old is None or old[2] < me[2]:
                t.r[me[0]] = me

    def op(self, E, fn, r=(), w=()):
        if self.dead:
            return None
        w = list(w) + [t for t in r if t.psum and t not in w]
        self._wait(E, self._deps(r, w))
        inst = fn(self.eng[E])
        self.cnt[E] += 1
        inst.then_inc(self.sem[E], 1)
        me = (E, self.sem[E], self.cnt[E], E)
        self._record(me, r, w)
        return inst

    def dma(self, E, out, in_, t, load=True, **kw):
        if self.dead:
            return None
        if load:
            self._wait(E, self._deps((), (t,)))
        else:
            self._wait(E, self._deps((t,), ()))
        if load:
            if t.dsem is None:
                t.dsem = self.nc.alloc_semaphore("dsem_%d" % len(self.dsems))
                self.dsems.append(t)
            inst = self.eng[E].dma_start(out=out, in_=in_, **kw)
            t.dcnt += 16
            inst.then_inc(t.dsem, 16)
            me = ("d_" + t.name, t.dsem, t.dcnt, None)
            self._record(me, (), (t,))
        else:
            if t.osem is None:
                t.osem = self.nc.alloc_semaphore("osem_%d" % len(self.osems))
                self.osems.append(t)
            inst = self.eng[E].dma_start(out=out, in_=in_, **kw)
            t.ocnt += 16
            inst.then_inc(t.osem, 16)
            me = ("o_" + t.name, t.osem, t.ocnt, None)
            self._record(me, (t,), ())
            self.outs.append(me)
        return inst

    def barrier(self):
        if self.dead:
            return
        deps = [(k, self.sem[k], self.cnt[k], k) for k in self.sem if self.cnt[k] > 0]
        for t in self.dsems:
            if t.dcnt > 0:
                deps.append(("d_" + t.name, t.dsem, t.dcnt, None))
        for t in self.osems:
            if t.ocnt > 0:
                deps.append(("o_" + t.name, t.osem, t.ocnt, None))
        for E in self.eng:
            self._wait(E, deps)

    def finish(self):
        self._wait("sp", self.outs)
        self.barrier()


class StopBuild(Exception):
    pass


def build(dbg=None, nb_run=NB, stop=None):
    nc = bass.Bass("TRN2", target_bir_lowering=False)
    S = Sync(nc)
    x = nc.dram_tensor("x", [NB, SEQ, DM], F32, kind="ExternalInput").ap()
    mem = nc.dram_tensor("mem", [NB, MEM, DM], F32, kind="ExternalInput").ap()
    w_in = nc.dram_tensor("w_in", [DM, INW], F32, kind="ExternalInput").ap()
    w_kv = nc.dram_tensor("w_kv", [DM, 1024], F32, kind="ExternalInput").ap()
    w_up = nc.dram_tensor("w_up", [3, 512, DM], F32, kind="ExternalInput").ap()
    w_out = nc.dram_tensor("w_out", [DM, DM], F32, kind="ExternalInput").ap()
    vecs = nc.dram_tensor("vecs", [128, 40], F32, kind="ExternalInput").ap()
    gfin = nc.dram_tensor("gfin", [128, DM], F32, kind="ExternalInput").ap()
    ropet = nc.dram_tensor("ropet", [128, 2 * SEQ], F32, kind="ExternalInput").ap()
    consts = nc.dram_tensor("consts", [128, C_END], F32, kind="ExternalInput").ap()
    y = nc.dram_tensor("y", [NB, SEQ, DM], F32, kind="ExternalOutput").ap()
    dbg_out = {}
    if dbg:
        for k, shp in dbg.items():
            dbg_out[k] = nc.dram_tensor("dbg_" + k, list(shp), F32, kind="ExternalOutput").ap()

    op = S.op
    root = ExitStack()
    with root:
        pA = S.tile(root, "pA", [128, 512], F32, psum=True)
        pB = S.tile(root, "pB", [128, 512], F32, psum=True)
        pR = S.tile(root, "pR", [128, 512], F32, psum=True)
        pS = [S.tile(root, "pS%d" % i, [128, 512], F32, psum=True) for i in range(2)]
        pV = [S.tile(root, "pV%d" % i, [128, 512], F32, psum=True) for i in range(2)]
        pT = S.tile(root, "pT", [128, 8, 128], BF16, psum=True)
        pAB = [pA, pB]
        cosT = S.tile(root, "cosT", [128, SEQ], F32)
        sinT = S.tile(root, "sinT", [128, SEQ], F32)
        cst = S.tile(root, "cst", [128, C_END - C_CBQ], F32)
        identb = S.tile(root, "identb", [128, 128], BF16)
        Rb = S.tile(root, "Rb", [128, 128], BF16)
        cbTb = S.tile(root, "cbTb", [128, 2, 256], BF16)
        vec = S.tile(root, "vec", [128, 40], F32)
        epst = S.tile(root, "epst", [128, 1], F32)
        wst = [S.tile(root, "wst%d" % i, [128, 512], F32) for i in range(2)]
        hT = S.tile(root, "hT", [128, 8, SEQ], BF16)
        smallf = S.tile(root, "smallf", [128, 64], F32)
        ss_v = [smallf.view("ss%d" % i) for i in range(4)]
        rs_v = [smallf.view("rs%d" % i) for i in range(4)]

        S.dma("sp", cosT.h[:, :], ropet[:, 0:SEQ], cosT)
        S.dma("sp", sinT.h[:, :], ropet[:, SEQ:2 * SEQ], sinT)
        S.dma("sp", cst.h[:, :], consts[:, C_CBQ:C_END], cst)
        S.dma("sp", vec.h[:, :], vecs[:, :], vec)
        with ExitStack() as st_s:
            cst0 = S.tile(st_s, "cst0", [128, C_CBQ], F32)
            S.dma("sp", cst0.h[:, :], consts[:, 0:C_CBQ], cst0)
            op("dve", lambda e: e.tensor_copy(identb.h[:, :], cst0.h[:, C_ID:C_ID + 128]), r=[cst0], w=[identb])
            op("dve", lambda e: e.tensor_copy(Rb.h[:, :], cst0.h[:, C_R:C_R + 128]), r=[cst0], w=[Rb])
            op("dve", lambda e: e.tensor_copy(cbTb.h[:, :, :], cst0.h[:, C_CBT:C_CBT + 512].rearrange("p (a b) -> p a b", a=2)),
               r=[cst0], w=[cbTb])
            S.barrier()
        op("dve", lambda e: e.memset(epst.h[:, :], 1e-6), w=[epst])
        wst_i = [0]
        scr_i = [0]
        xi_i = [0]
        xst_i = [0]

        def load_w(dst, ncols, src_fn, stg=None):
            nchunk = dst.h.shape[1]
            if stg is None:
                stg = wst
            for c in range(nchunk):
                for b0 in range(0, ncols, 512):
                    bw = min(512, ncols - b0)
                    st = stg[wst_i[0] % len(stg)]
                    wst_i[0] += 1
                    S.dma("sp", st.h[:, 0:bw], src_fn(c)[:, b0:b0 + bw], st)
                    if wst_i[0] % 2 == 0:
                        op("act", lambda e: e.activation(out=dst.h[:, c, b0:b0 + bw], in_=st.h[:, 0:bw], func=AF.Copy), r=[st], w=[dst])
                    else:
                        op("dve", lambda e: e.tensor_copy(dst.h[:, c, b0:b0 + bw], st.h[:, 0:bw]), r=[st], w=[dst])

        def win_src(col0, ncols):
            return lambda c: w_in[c * 128:(c + 1) * 128, col0:col0 + ncols]

        def rmsnorm_T(src_fn, ntiles, dstT, gcol, xst, xn):
            for tt in range(ntiles):
                xs = xst[xst_i[0] % len(xst)]
                xnn = xn[xst_i[0] % len(xn)]
                xst_i[0] += 1
                k = tt % 4
                ssv, rsv = ss_v[k], rs_v[k]
                S.dma("sp", xs.h[:, :], src_fn(tt), xs)
                op("act", lambda e: e.activation(out=xnn.h[:, :], in_=xs.h[:, :], func=AF.Square,
                                                 accum_out=smallf.h[:, k:k + 1]), r=[xs], w=[xnn, ssv])
                op("act", lambda e: e.activation(out=smallf.h[:, 4 + k:5 + k], in_=smallf.h[:, k:k + 1], func=AF.Sqrt,
                                                 scale=1.0 / DM, bias=epst.h[:, 0:1]), r=[ssv, epst], w=[rsv])
                op("dve", lambda e: e.reciprocal(smallf.h[:, 4 + k:5 + k], smallf.h[:, 4 + k:5 + k]), r=[rsv], w=[rsv])
                op("dve", lambda e: e.tensor_scalar(out=xnn.h[:, :], in0=xs.h[:, :], scalar1=smallf.h[:, 4 + k:5 + k], scalar2=None,
                                                    op0=ALU.mult), r=[xs, rsv], w=[xnn])
                for c in range(8):
                    op("pe", lambda e: e.transpose(pT.h[:, c, :], xnn.h[:, c * 128:(c + 1) * 128], identb.h[:, :]),
                       r=[xnn, identb], w=[pT])
                op("dve", lambda e: e.tensor_tensor(out=dstT.h[:, :, tt * 128:(tt + 1) * 128], in0=pT.h[:, :, :],
                                                    in1=vec.h[:, gcol:gcol + 8].unsqueeze(2).to_broadcast([128, 8, 128]), op=ALU.mult),
                   r=[pT, vec], w=[dstT])

        ab_i = [0]

        def proj_fm(W, col0, rhsT, t0, n, m=128, ps=None):
            if ps is None:
                ps = pAB[ab_i[0] % 2]
                ab_i[0] += 1
            nchunk = W.h.shape[1]
            for c in range(nchunk):
                op("pe", lambda e: e.matmul(ps.h[0:m, 0:n], lhsT=W.h[:, c, col0:col0 + m], rhs=rhsT.h[:, c, t0:t0 + n],
                                            start=(c == 0), stop=(c == nchunk - 1)), r=[W, rhsT], w=[ps])
            return ps

        def proj_tm(W, col0, ncols, lhsT_, t0, ps=None):
            if ps is None:
                ps = pAB[ab_i[0] % 2]
                ab_i[0] += 1
            nchunk = W.h.shape[1]
            for c in range(nchunk):
                op("pe", lambda e: e.matmul(ps.h[:, 0:ncols], lhsT=lhsT_.h[:, c, t0:t0 + 128], rhs=W.h[:, c, col0:col0 + ncols],
                                            start=(c == 0), stop=(c == nchunk - 1)), r=[W, lhsT_], w=[ps])
            return ps

        def rope(ps, n, pos0, dst_ap, dst, scr):
            if isinstance(scr, list):
                scr_i[0] += 1
                scr = scr[scr_i[0] % len(scr)]
            zb, t1, t2 = scr
            ck("kp")
            op("act", lambda e: e.activation(out=zb.h[:, 0:n], in_=ps.h[:, 0:n], func=AF.Copy), r=[ps], w=[zb])
            ck("kr0")
            op("pe", lambda e: e.matmul(pR.h[:, 0:n], lhsT=Rb.h[:, :], rhs=zb.h[:, 0:n], start=True, stop=True),
               r=[Rb, zb], w=[pR])
            ck("kr1")
            op("dve", lambda e: e.tensor_tensor(out=t1.h[:, 0:n], in0=ps.h[:, 0:n], in1=cosT.h[:, pos0:pos0 + n], op=ALU.mult),
               r=[ps, cosT], w=[t1])
            op("dve", lambda e: e.tensor_tensor(out=t2.h[:, 0:n], in0=pR.h[:, 0:n], in1=sinT.h[:, pos0:pos0 + n], op=ALU.mult),
               r=[pR, sinT], w=[t2])
            ck("kr2")
            if isinstance(dst_ap, tuple):
                op("pool", lambda e: e.tensor_tensor(out=dst_ap[0], in0=t1.h[0:64, 0:n], in1=t2.h[0:64, 0:n], op=ALU.add),
                   r=[t1, t2], w=[dst])
                op("pool", lambda e: e.tensor_tensor(out=dst_ap[1], in0=t1.h[64:128, 0:n], in1=t2.h[64:128, 0:n], op=ALU.add),
                   r=[t1, t2], w=[dst])
            else:
                op("pool", lambda e: e.tensor_tensor(out=dst_ap, in0=t1.h[:, 0:n], in1=t2.h[:, 0:n], op=ALU.add),
                   r=[t1, t2], w=[dst])

        def dump(name, t, ap, shape):
            if name not in dbg_out:
                return
            with ExitStack() as es:
                tmp = S.tile(es, "dbgtmp", [128, 512], F32)
                if len(shape) == 2:
                    pieces = [(ap[:, c0:min(c0 + 512, shape[1])], dbg_out[name][:, c0:min(c0 + 512, shape[1])], min(512, shape[1] - c0))
                              for c0 in range(0, shape[1], 512)]
                else:
                    pieces = [(ap[:, a, c0:min(c0 + 512, shape[2])], dbg_out[name][:, a, c0:min(c0 + 512, shape[2])], min(512, shape[2] - c0))
                              for a in range(shape[1]) for c0 in range(0, shape[2], 512)]
                for src, dst, n in pieces:
                    op("dve", lambda e: e.tensor_copy(tmp.h[:, 0:n], src), r=[t], w=[tmp])
                    S.dma("sp", dst, tmp.h[:, 0:n], tmp, load=False)
                S.barrier()

        def attn_out(acc_fn, sg, abt, aT, tok0, nheads, hd):
            pass

        def ck(name):
            if stop == name:
                S.barrier()
                S.dead = True

        try:
          for b in range(nb_run):
              with ExitStack() as st0:
                  xst = [S.tile(st0, "xst%d" % i, [128, 1024], F32) for i in range(4)]
                  xn = [S.tile(st0, "xn%d" % i, [128, 1024], BF16) for i in range(4)]
                  rmsnorm_T(lambda tt: x[b, tt * 128:(tt + 1) * 128, :], SEQ // 128, hT, 0, xst, xn)
                  S.barrier()
              if b == 0:
                  dump("hT", hT, hT.h[:, :, :], (128, 8, SEQ))
              ck("stage0")

              with ExitStack() as st_a:
                  aT_dsa = S.tile(st_a, "aT_dsa", [128, 4, SEQ], BF16)
                  with ExitStack() as st:
                      KT = S.tile(st, "KT", [128, 4, SEQ], BF16)
                      VA = S.tile(st, "VA", [128, 16, 8, 65], BF16)
                      wA = S.tile(st, "wA", [128, 8, 512], BF16)
                      wB = S.tile(st, "wB", [128, 8, 512], BF16)
                      wC = S.tile(st, "wC", [128, 8, 512], BF16)
                      wI = S.tile(st, "wI", [128, 8, 8], BF16)
                      ikT = S.tile(st, "ikT", [128, SEQ], BF16)
                      stk = ExitStack()
                      zb = S.tile(stk, "zb", [128, 512], BF16)
                      t1 = S.tile(stk, "t1", [128, 512], F32)
                      t2 = S.tile(stk, "t2", [128, 512], F32)
                      scr = (zb, t1, t2)
                      wst6 = wst + [S.tile(stk, "wstk%d" % i, [128, 512], F32) for i in range(4)]
                      load_w(wA, 512, win_src(OFF["dk"], 512), stg=wst6)
                      load_w(wB, 512, win_src(OFF["dv"], 512), stg=wst6)
                      load_w(wC, 64, win_src(OFF["ik"], 64), stg=wst6)
                      ck("kw")
                      op("pool", lambda e: e.tensor_copy(wC.h[:, :, 64:128], wC.h[:, :, 0:64]), r=[wC], w=[wC])
                      op("pool", lambda e: e.memset(VA.h[:, :, :, 64:65], 1.0), w=[VA])
                      ck("kw2")
                      for tg in range(4):
                          for pr in range(4):
                              ps = proj_fm(wA, pr * 128, hT, tg * 512, 512)
                              rope(ps, 512, tg * 512, KT.h[:, pr, tg * 512:(tg + 1) * 512], KT, scr)
                              ck("kr")
                          ps = proj_fm(wC, 0, hT, tg * 512, 512)
                          rope(ps, 512, tg * 512, ikT.h[:, tg * 512:(tg + 1) * 512], ikT, scr)
                          for t4 in range(4):
                              tt = tg * 4 + t4
                              ps = proj_tm(wB, 0, 512, hT, tt * 128)
                              op("act", lambda e: e.activation(out=VA.h[:, tt, :, 0:64],
                                                               in_=ps.h[:, :].rearrange("p (h d) -> p h d", h=8),
                                                               func=AF.Copy), r=[ps], w=[VA])
                      if b == 0:
                          dump("dKT", KT, KT.h[:, :, :], (128, 4, SEQ))
                          dump("ikT", ikT, ikT.h[:, :], (128, SEQ))
                      ck("kside")
                      S.barrier()
                      stk.close()
                      sgall = S.tile(st, "sgall", [128, 16, 512], BF16)
                      with ExitStack() as stq:
                          wst6 = wst + [S.tile(stq, "wstq%d" % i, [128, 512], F32) for i in range(4)]
                          load_w(wB, 512, win_src(OFF["dg"], 512), stg=wst6)
                          load_w(wA, 512, win_src(OFF["dq"], 512), stg=wst6)
                          load_w(wC, 512, win_src(OFF["iq"], 512), stg=wst6)
                          load_w(wI, 8, win_src(OFF["iw"], 8), stg=wst6)
                          for tt in range(16):
                              ps = proj_tm(wB, 0, 512, hT, tt * 128)
                              op("act", lambda e: e.activation(out=sgall.h[:, tt, :], in_=ps.h[:, :], func=AF.Silu), r=[ps], w=[sgall])
                          S.barrier()
                      zb = S.tile(st, "zbq", [128, 256], BF16)
                      t1 = S.tile(st, "t1q", [128, 256], F32)
                      t2 = S.tile(st, "t2q", [128, 256], F32)
                      scr = (zb, t1, t2)
                      accS = S.tile(st, "accS", [128, 8, 2, 65], F32)
                      rdv = S.tile(st, "rdv", [128, 8, 2], F32)
                      QT = [S.tile(st, "QT%d" % i, [128, 4, 2, 256], BF16) for i in range(2)]
                      for i in range(2):
                          op("pool", lambda e: e.memset(QT[i].h[:, :, :, :], 0.0), w=[QT[i]])
                      iqT = S.tile(st, "iqT", [128, 4, 256], BF16)
                      abt = [S.tile(st, "abt%d" % i, [128, 512], BF16) for i in range(2)]
                      rl = [S.tile(st, "rl%d" % i, [128, 512], F32) for i in range(2)]
                      sc = [S.tile(st, "sc%d" % i, [128, SEQ], F32) for i in range(2)]
                      mb = [[S.tile(st, "mb%d_%d" % (j, i), [128, SEQ], BF16) for i in range(2)] for j in range(2)]
                      cnts = S.tile(st, "cnts", [128, 4], F32)
                      cnt0 = cnts.view("c0")
                      cnt1 = cnts.view("c1")
                      cthr = cnts.view("thr")
                      PT = [S.tile(st, "PT%d" % i, [128, 2, 256], BF16) for i in range(2)]
                      iws = S.tile(st, "iws", [128, 2, 8], F32)
                      bis = S.tile(st, "bis", [128, 16], F32)
                      wk_all = S.tile(st, "wk_all", [128, 2, NITER], F32)
                      gem = S.tile(st, "gem", [128, 2], U32)
                      rd = S.tile(st, "rd", [128, 4], F32)

                      op("dve", lambda e: e.memset(bis.h[:, :], 0.0), w=[bis])

                      def X_chunks(m):
                          N = 256 * (m + 1)
                          tok0 = 256 * m
                          bi = m % 2
                          ch = []

                          def c_proj(pr):
                              ps = proj_fm(wA, pr * 128, hT, tok0, 256)
                              rope(ps, 256, tok0, (QT[bi].h[0:64, pr, 0, :], QT[bi].h[64:128, pr, 1, :]), QT[bi], scr)
                              ps = proj_fm(wC, pr * 128, hT, tok0, 256)
                              rope(ps, 256, tok0, iqT.h[:, pr, :], iqT, scr)
                          for pr in range(4):
                              ch.append(lambda pr=pr: c_proj(pr))

                          def c_gate(t):
                              ps = proj_tm(wI, 0, 8, hT, tok0 + t * 128)
                              op("dve", lambda e: e.tensor_scalar(out=iws.h[:, t, :], in0=ps.h[:, 0:8], scalar1=0.125 * (8 ** -0.5),
                                                                  scalar2=None, op0=ALU.mult), r=[ps], w=[iws])
                          ch.append(lambda: (c_gate(0), c_gate(1)))

                          def c_idx(t, k0):
                              kw = min(512, N - k0)
                              for g in range(8):
                                  p0 = (g % 2) * 64
                                  pss = [pA, pB, pR][xi_i[0] % 3]
                                  rlt = rl[xi_i[0] % 2]
                                  xi_i[0] += 1
                                  op("pe", lambda e: e.matmul(pss.h[:, 0:kw], lhsT=iqT.h[p0:p0 + 64, g // 2, t * 128:(t + 1) * 128],
                                                              rhs=ikT.h[p0:p0 + 64, k0:k0 + kw], start=True, stop=True),
                                     r=[iqT, ikT], w=[pss])
                                  op("act", lambda e: e.activation(out=rlt.h[:, 0:kw], in_=pss.h[:, 0:kw], func=AF.Relu),
                                     r=[pss], w=[rlt])
                                  if g == 0:
                                      op("dve", lambda e: e.tensor_scalar(out=sc[t].h[:, k0:k0 + kw], in0=rlt.h[:, 0:kw],
                                                                          scalar1=iws.h[:, t, 0:1], scalar2=None, op0=ALU.mult),
                                         r=[rlt, iws], w=[sc[t]])
                                  else:
                                      op("dve", lambda e: e.scalar_tensor_tensor(out=sc[t].h[:, k0:k0 + kw], in0=rlt.h[:, 0:kw],
                                                                                 scalar=iws.h[:, t, g:g + 1], in1=sc[t].h[:, k0:k0 + kw],
                                                                                 op0=ALU.mult, op1=ALU.add),
                                         r=[rlt, iws, sc[t]], w=[sc[t]])
                          for t in range(2):
                              for k0 in range(0, N, 512):
                                  ch.append(lambda t=t, k0=k0: c_idx(t, k0))

                          def c_pre():
                              if m == 0:
                                  op("dve", lambda e: e.memset(bis.h[:, 6:8], -1.0e29), w=[bis])
                              else:
                                  op("dve", lambda e: e.memset(cnts.h[:, 2:3], float(N - 256)), w=[cthr])
                                  op("dve", lambda e: e.memset(cnts.h[:, 3:4], float(N - 512)), w=[cthr])
                                  for t in range(2):
                                      op("dve", lambda e: e.tensor_reduce(out=bis.h[:, 10 + t:11 + t], in_=sc[t].h[:, 0:N], axis=AX.X, op=ALU.max,
                                                                          apply_absolute_value=True), r=[sc[t]], w=[bis])
                                  op("dve", lambda e: e.tensor_scalar(out=bis.h[:, 0:2], in0=bis.h[:, 10:12], scalar1=-1.0, scalar2=None, op0=ALU.mult),
                                     r=[bis], w=[bis])
                                  op("dve", lambda e: e.tensor_scalar(out=bis.h[:, 8:10], in0=bis.h[:, 10:12], scalar1=2.0, scalar2=None, op0=ALU.mult),
                                     r=[bis], w=[bis])
                                  op("dve", lambda e: e.tensor_tensor(out=wk_all.h[:, :, :],
                                                                      in0=bis.h[:, 8:10].unsqueeze(2).to_broadcast([128, 2, NITER]),
                                                                      in1=cst.h[:, C_PW - C_CBQ:C_PW - C_CBQ + NITER].unsqueeze(1).to_broadcast([128, 2, NITER]),
                                                                      op=ALU.mult), r=[bis, cst], w=[wk_all])
                              for t in range(2):
                                  op("dve", lambda e: e.tensor_tensor(out=sc[t].h[:, N - 256:N], in0=sc[t].h[:, N - 256:N],
                                                                      in1=cst.h[:, t * 256:(t + 1) * 256], op=ALU.add),
                                     r=[sc[t], cst], w=[sc[t]])
                          ch.append(c_pre)

                          def c_iter(it):
                              op("dve", lambda e: e.tensor_tensor(out=bis.h[:, 2:4], in0=bis.h[:, 0:2], in1=wk_all.h[:, :, it], op=ALU.add),
                                 r=[bis, wk_all], w=[bis])
                              op("act", lambda e: e.activation(out=mb[bi][1].h[:, 0:N], in_=sc[1].h[:, 0:N], func=AF.Sign, bias=bis.h[:, 3:4],
                                                               scale=-1.0, accum_out=cnts.h[:, 1:2]), r=[sc[1], bis], w=[mb[bi][1], cnt1])
                              op("dve", lambda e: e.tensor_scalar(out=mb[bi][0].h[:, 0:N], in0=sc[0].h[:, 0:N], scalar1=bis.h[:, 2:3],
                                                                  scalar2=None, op0=ALU.is_lt, op1=ALU.add,
                                                                  accum_out=cnts.h[:, 0:1]), r=[sc[0], bis], w=[mb[bi][0], cnt0])
                              op("dve", lambda e: e.tensor_tensor(out=gem.h[:, :], in0=cnts.h[:, 0:2], in1=cnts.h[:, 2:4], op=ALU.is_le),
                                 r=[cnt0, cnt1, cthr], w=[gem])
                              op("dve", lambda e: e.copy_predicated(bis.h[:, 0:2], gem.h[:, :], bis.h[:, 2:4]), r=[gem, bis], w=[bis])
                          if m > 0:
                              for it in range(NITER):
                                  ch.append(lambda it=it: c_iter(it))

                          def c_fin():
                              if m > 0:
                                  op("dve", lambda e: e.tensor_copy(bis.h[:, 6:8], bis.h[:, 0:2]), r=[bis], w=[bis])
                              for t in range(2):
                                  op("dve", lambda e: e.tensor_scalar(out=mb[bi][t].h[:, 0:N], in0=sc[t].h[:, 0:N], scalar1=bis.h[:, 6 + t:7 + t],
                                                                      scalar2=NEG, op0=ALU.is_lt, op1=ALU.mult), r=[sc[t], bis], w=[mb[bi][t]])
                          ch.append(c_fin)
                          return ch

                      for f in X_chunks(0):
                          f()
                      dsa_tail = [None]
                      for m in range(8):
                          N = 256 * (m + 1)
                          tok0 = 256 * m
                          bi = m % 2
                          nkt = N // 128
                          steps = [(h, jj) for h in range(8) for jj in range(nkt // 2)]
                          xc = X_chunks(m + 1) if m + 1 < 8 else []

                          def d_qk(i):
                              h, jj = steps[i]
                              p0 = (h % 2) * 64
                              pr = h // 2
                              pss = pS[i % 2]
                              for jl in range(2):
                                  j = jj * 2 + jl
                                  op("pe", lambda e: e.matmul(pss.h[:, jl * 256:(jl + 1) * 256], lhsT=KT.h[:, pr, j * 128:(j + 1) * 128],
                                                              rhs=QT[bi].h[:, pr, h % 2, :], start=True, stop=False), r=[KT, QT[bi]], w=[pss])
                                  for t in range(2):
                                      op("pe", lambda e: e.matmul(pss.h[:, jl * 256 + t * 128:jl * 256 + (t + 1) * 128],
                                                                  lhsT=mb[bi][t].h[:, j * 128:(j + 1) * 128], rhs=identb.h[:, :],
                                                                  start=False, stop=(t == 1)), r=[mb[bi][t], identb], w=[pss])

                          def d_rest(i):
                              h, jj = steps[i]
                              pss = pS[i % 2]
                              ptt = PT[i % 2]
                              acc = pV[h % 2]
                              op("act", lambda e: e.activation(out=ptt.h[:, :, :], in_=pss.h[:, :].rearrange("p (a b) -> p a b", a=2),
                                                               func=AF.Exp, scale=0.125), r=[pss], w=[ptt])
                              for jl in range(2):
                                  j = jj * 2 + jl
                                  for t in range(2):
                                      op("pe", lambda e: e.matmul(acc.h[:, t * 65:(t + 1) * 65], lhsT=ptt.h[:, jl, t * 128:(t + 1) * 128],
                                                                  rhs=VA.h[:, j, h, :], start=(j == 0 and t == 0), stop=(j == nkt - 1),
                                                                  skip_group_check=True),
                                         r=[ptt, VA], w=[acc])
                              if jj == nkt // 2 - 1:
                                  op("act", lambda e: e.activation(out=accS.h[:, h, :, :], in_=acc.h[:, 0:130].rearrange("p (t d) -> p t d", t=2),
                                                                   func=AF.Copy), r=[acc], w=[accS])

                          d_qk(0)
                          if dsa_tail[0] is not None:
                              dsa_tail[0]()
                          done = 0
                          for i in range(len(steps)):
                              if i + 1 < len(steps):
                                  d_qk(i + 1)
                              d_rest(i)
                              target = ((i + 1) * len(xc) + len(steps) - 1) // len(steps)
                              while done < target:
                                  xc[done]()
                                  done += 1
                          def make_dsa_tail(tok0=tok0):
                              def tail():
                                  op("dve", lambda e: e.reciprocal(rdv.h[:, :, :], accS.h[:, :, :, 64]), r=[accS], w=[rdv])
                                  for t in range(2):
                                      op("dve", lambda e: e.tensor_tensor(out=rl[t].h[:, :].rearrange("p (h d) -> p h d", h=8), in0=accS.h[:, :, t, 0:64],
                                                                          in1=rdv.h[:, :, t].unsqueeze(2).to_broadcast([128, 8, 64]), op=ALU.mult),
                                         r=[accS, rdv], w=[rl[t]])
                                      op("pool", lambda e: e.tensor_tensor(out=abt[t].h[:, :], in0=rl[t].h[:, :], in1=sgall.h[:, (tok0 // 128) + t, :], op=ALU.mult),
                                         r=[rl[t], sgall], w=[abt[t]])
                                      for c in range(4):
                                          op("pe", lambda e: e.transpose(pT.h[:, c, :], abt[t].h[:, c * 128:(c + 1) * 128], identb.h[:, :]),
                                             r=[abt[t], identb], w=[pT])
                                      op("dve", lambda e: e.tensor_copy(aT_dsa.h[:, :, tok0 + t * 128:tok0 + (t + 1) * 128], pT.h[:, 0:4, :]),
                                         r=[pT], w=[aT_dsa])
                              return tail
                          dsa_tail[0] = make_dsa_tail()
                      dsa_tail[0]()
                      S.barrier()
                  if b == 0:
                      dump("aT_dsa", aT_dsa, aT_dsa.h[:, :, :], (128, 4, SEQ))
                  S.barrier()
                  ck("dsa")
                  with ExitStack() as st_b:
                      aT_moba = S.tile(st_b, "aT_moba", [128, 4, SEQ], BF16)
                      with ExitStack() as st:
                          KT = S.tile(st, "KT", [128, 4, SEQ], BF16)
                          VA = S.tile(st, "VA", [128, 16, 8, 65], BF16)
                          wA = S.tile(st, "wA", [128, 8, 512], BF16)
                          wB = S.tile(st, "wB", [128, 8, 512], BF16)
                          scr = [(S.tile(st, "zb%d" % i, [128, 512], BF16), S.tile(st, "t1_%d" % i, [128, 512], F32),
                                  S.tile(st, "t2_%d" % i, [128, 512], F32)) for i in range(2)]
                          QT = [S.tile(st, "QT%d" % i, [128, 4, 2, 256], BF16) for i in range(2)]
                          QTg = S.tile(st, "QTg", [128, 4, 256], BF16)
                          for i in range(2):
                              op("pool", lambda e: e.memset(QT[i].h[:, :, :, :], 0.0), w=[QT[i]])
                          sgall = S.tile(st, "sgall", [128, 16, 512], BF16)
                          yb = [S.tile(st, "yb%d" % i, [128, 512], F32) for i in range(2)]
                          abt = [S.tile(st, "abt%d" % i, [128, 512], BF16) for i in range(2)]
                          PT = [S.tile(st, "PT%d" % i, [128, 2, 256], BF16) for i in range(3)]
                          pS3 = [pS[0], pS[1], pB]
                          kmf = S.tile(st, "kmf", [128, 4, 8], F32)
                          kmT = S.tile(st, "kmT", [128, 4, 16], BF16)
                          gt = S.tile(st, "gt", [128, 2, 8, 8], F32)
                          cmpb = S.tile(st, "cmpb", [128, 16, 8, 8], F32)
                          rank = S.tile(st, "rank", [128, 16, 8], F32)
                          sel = [S.tile(st, "sel%d" % i, [128, 2, 8, 8], F32) for i in range(2)]
                          accs = [S.tile(st, "accs%d" % i, [128, 8, 65], F32) for i in range(2)]
                          rdv = S.tile(st, "rdv", [128, 2, 8], F32)
                          wst6 = wst + [S.tile(st, "wstm%d" % i, [128, 512], F32) for i in range(4)]
                          load_w(wA, 512, win_src(OFF["mk"], 512), stg=wst6)
                          load_w(wB, 512, win_src(OFF["mv"], 512), stg=wst6)
                          op("pool", lambda e: e.memset(VA.h[:, :, :, 64:65], 1.0), w=[VA])
                          for tg in range(4):
                              for pr in range(4):
                                  ps = proj_fm(wA, pr * 128, hT, tg * 512, 512)
                                  rope(ps, 512, tg * 512, KT.h[:, pr, tg * 512:(tg + 1) * 512], KT, scr)
                              for t4 in range(4):
                                  tt = tg * 4 + t4
                                  ps = proj_tm(wB, 0, 512, hT, tt * 128)
                                  op("act", lambda e: e.activation(out=VA.h[:, tt, :, 0:64],
                                                                   in_=ps.h[:, :].rearrange("p (h d) -> p h d", h=8),
                                                                   func=AF.Copy), r=[ps], w=[VA])
                          ck("mk")
                          for pr in range(4):
                              op("dve", lambda e: e.tensor_reduce(out=kmf.h[:, pr, :], in_=KT.h[:, pr, :].rearrange("p (n k) -> p n k", n=8),
                                                                  axis=AX.X, op=ALU.add), r=[KT], w=[kmf])
                          op("dve", lambda e: e.memset(kmT.h[:, :, :], 0.0), w=[kmT])
                          op("dve", lambda e: e.tensor_scalar(out=kmT.h[0:64, :, 0:8], in0=kmf.h[0:64, :, :], scalar1=1.0 / 256.0, scalar2=None,
                                                              op0=ALU.mult), r=[kmf], w=[kmT])
                          op("dve", lambda e: e.tensor_scalar(out=kmT.h[64:128, :, 8:16], in0=kmf.h[64:128, :, :], scalar1=1.0 / 256.0, scalar2=None,
                                                              op0=ALU.mult), r=[kmf], w=[kmT])
                          ck("mkm")
                          load_w(wB, 512, win_src(OFF["mg"], 512), stg=wst6)
                          load_w(wA, 512, win_src(OFF["mq"], 512), stg=wst6)
                          for tt in range(16):
                              ps = proj_tm(wB, 0, 512, hT, tt * 128)
                              op("act", lambda e: e.activation(out=sgall.h[:, tt, :], in_=ps.h[:, :], func=AF.Silu), r=[ps], w=[sgall])
                          def MX_chunks(m):
                              tok0 = 256 * m
                              bi = m % 2
                              ch = []

                              def c_proj(pr):
                                  ps = proj_fm(wA, pr * 128, hT, tok0, 256, ps=pA)
                                  rope(ps, 256, tok0, (QT[bi].h[0:64, pr, 0, :], QT[bi].h[64:128, pr, 1, :]), QT[bi], scr)
                              for pr in range(4):
                                  ch.append(lambda pr=pr: c_proj(pr))

                              def c_sel0():
                                  if m <= 3:
                                      op("dve", lambda e: e.memset(sel[bi].h[:, :, :, :], 1.0), w=[sel[bi]])
                                      return
                                  op("pool", lambda e: e.tensor_tensor(out=QTg.h[:, :, :], in0=QT[bi].h[:, :, 0, :], in1=QT[bi].h[:, :, 1, :], op=ALU.add),
                                     r=[QT[bi]], w=[QTg])
                                  for t in range(2):
                                      for pr in range(4):
                                          op("pe", lambda e: e.matmul(pR.h[:, t * 64 + pr * 16:t * 64 + pr * 16 + 16],
                                                                      lhsT=QTg.h[:, pr, t * 128:(t + 1) * 128],
                                                                      rhs=kmT.h[:, pr, :], start=True, stop=True),
                                             r=[QTg, kmT], w=[pR])
                                  op("dve", lambda e: e.tensor_copy(gt.h[:, :, :, :], pR.h[:, 0:128].rearrange("p (t h n) -> p t h n", t=2, h=8)),
                                     r=[pR], w=[gt])
                                  op("dve", lambda e: e.memset(gt.h[:, :, :, m:8], NEGF), w=[gt])
                              ch.append(c_sel0)

                              def c_sel1():
                                  g3 = gt.h[:, :, :, :].rearrange("p t h n -> p (t h) n")
                                  op("dve", lambda e: e.tensor_tensor(out=cmpb.h[:, :, :, :], in0=g3.unsqueeze(2).to_broadcast([128, 16, 8, 8]),
                                                                      in1=g3.unsqueeze(3).to_broadcast([128, 16, 8, 8]), op=ALU.is_gt),
                                     r=[gt], w=[cmpb])
                                  op("dve", lambda e: e.tensor_reduce(out=rank.h[:, :, :], in_=cmpb.h[:, :, :, :], axis=AX.X, op=ALU.add),
                                     r=[cmpb], w=[rank])
                                  op("dve", lambda e: e.tensor_scalar(out=sel[bi].h[:, :, :, :].rearrange("p t h n -> p (t h) n"), in0=rank.h[:, :, :],
                                                                      scalar1=3.0, scalar2=None, op0=ALU.is_lt), r=[rank], w=[sel[bi]])
                                  op("dve", lambda e: e.memset(sel[bi].h[:, :, :, m:m + 1], 1.0), w=[sel[bi]])
                              if m > 3:
                                  ch.append(c_sel1)
                              return ch

                          for f in MX_chunks(0):
                              f()
                          pending_tail = [None]
                          for m in range(8):
                              tok0 = 256 * m
                              bi = m % 2
                              xc = MX_chunks(m + 1) if m + 1 < 8 else []
                              steps = [(h, n) for h in range(8) for n in range(m + 1)]

                              def m_qk(i):
                                  h, n = steps[i]
                                  pr = h // 2
                                  pss = pS3[i % 3]
                                  for jl in range(2):
                                      j = 2 * n + jl
                                      op("pe", lambda e: e.matmul(pss.h[:, jl * 256:(jl + 1) * 256], lhsT=KT.h[:, pr, j * 128:(j + 1) * 128],
                                                                  rhs=QT[bi].h[:, pr, h % 2, :], start=True, stop=(n < m)), r=[KT, QT[bi]], w=[pss])
                                      if n == m:
                                          op("pe", lambda e: e.matmul(pss.h[:, jl * 256:(jl + 1) * 256], lhsT=identb.h[:, :], rhs=cbTb.h[:, jl, :],
                                                                      start=False, stop=True), r=[identb, cbTb], w=[pss])

                              def m_rest(i):
                                  h, n = steps[i]
                                  pss = pS3[i % 3]
                                  ptt = PT[i % 3]
                                  acc = pV[i % 2]
                                  op("act", lambda e: e.activation(out=ptt.h[:, :, :], in_=pss.h[:, :].rearrange("p (a b) -> p a b", a=2),
                                                                   func=AF.Exp, scale=0.125), r=[pss], w=[ptt])
                                  for jl in range(2):
                                      j = 2 * n + jl
                                      for t in range(2):
                                          op("pe", lambda e: e.matmul(acc.h[:, t * 65:(t + 1) * 65], lhsT=ptt.h[:, jl, t * 128:(t + 1) * 128],
                                                                      rhs=VA.h[:, j, h, :], start=(jl == 0 and t == 0), stop=(jl == 1),
                                                                      skip_group_check=True), r=[ptt, VA], w=[acc])
                                  for t in range(2):
                                      if n == 0:
                                          op("dve", lambda e: e.tensor_scalar(out=accs[t].h[:, h, :], in0=acc.h[:, t * 65:(t + 1) * 65],
                                                                              scalar1=sel[bi].h[:, t, h, 0:1], scalar2=None, op0=ALU.mult),
                                             r=[acc, sel[bi]], w=[accs[t]])
                                      else:
                                          op("dve", lambda e: e.scalar_tensor_tensor(out=accs[t].h[:, h, :], in0=acc.h[:, t * 65:(t + 1) * 65],
                                                                                     scalar=sel[bi].h[:, t, h, n:n + 1], in1=accs[t].h[:, h, :],
                                                                                     op0=ALU.mult, op1=ALU.add),
                                             r=[acc, sel[bi], accs[t]], w=[accs[t]])

                              m_qk(0)
                              if len(steps) > 1:
                                  m_qk(1)
                              if pending_tail[0] is not None:
                                  pending_tail[0]()
                              done = 0
                              for i in range(len(steps)):
                                  if i + 2 < len(steps):
                                      m_qk(i + 2)
                                  m_rest(i)
                                  target = ((i + 1) * len(xc) + len(steps) - 1) // len(steps)
                                  while done < target:
                                      xc[done]()
                                      done += 1
                              def make_tail(tok0=tok0, bi=bi):
                                  def tail():
                                      for t in range(2):
                                          op("dve", lambda e: e.reciprocal(rdv.h[:, t, :], accs[t].h[:, :, 64]), r=[accs[t]], w=[rdv])
                                          op("dve", lambda e: e.tensor_tensor(out=yb[t].h[:, :].rearrange("p (h d) -> p h d", h=8), in0=accs[t].h[:, :, 0:64],
                                                                              in1=rdv.h[:, t, :].unsqueeze(2).to_broadcast([128, 8, 64]), op=ALU.mult),
                                             r=[accs[t], rdv], w=[yb[t]])
                                          op("dve", lambda e: e.tensor_tensor(out=abt[t].h[:, :], in0=yb[t].h[:, :], in1=sgall.h[:, (tok0 // 128) + t, :], op=ALU.mult),
                                             r=[yb[t], sgall], w=[abt[t]])
                                          for c in range(4):
                                              op("pe", lambda e: e.transpose(pT.h[:, c, :], abt[t].h[:, c * 128:(c + 1) * 128], identb.h[:, :]),
                                                 r=[abt[t], identb], w=[pT])
                                          op("dve", lambda e: e.tensor_copy(aT_moba.h[:, :, tok0 + t * 128:tok0 + (t + 1) * 128], pT.h[:, 0:4, :]),
                                             r=[pT], w=[aT_moba])
                                  return tail
                              pending_tail[0] = make_tail()
                          pending_tail[0]()
                          S.barrier()
                      if b == 0:
                          dump("aT_moba", aT_moba, aT_moba.h[:, :, :], (128, 4, SEQ))
                      S.barrier()
                      ck("moba")
                      with ExitStack() as st_c:
                          aT_x = S.tile(st_c, "aT_x", [128, 4, SEQ], BF16)
                          with ExitStack() as st:
                              memT = S.tile(st, "memT", [128, 8, MEM], BF16)
                              wK = S.tile(st, "wK", [128, 8, 1024], BF16)
                              xkT = S.tile(st, "xkT", [128, 4, MEM], BF16)
                              xva = S.tile(st, "xva", [128, 2, 4, 129], BF16)
                              wA = S.tile(st, "wA", [128, 8, 512], BF16)
                              wB = S.tile(st, "wB", [128, 8, 512], BF16)
                              xqT2 = [S.tile(st, "xqT%d" % i, [128, 4, 256], BF16) for i in range(2)]
                              sg2 = [[S.tile(st, "sg%d_%d" % (j, i), [128, 512], F32) for i in range(2)] for j in range(2)]
                              yb2 = [[S.tile(st, "yb%d_%d" % (j, i), [128, 512], F32) for i in range(2)] for j in range(2)]
                              abt = [S.tile(st, "abt%d" % i, [128, 512], BF16) for i in range(2)]
                              PT = [S.tile(st, "PT%d" % i, [128, 2, 256], BF16) for i in range(2)]
                              rd = S.tile(st, "rd", [128, 4], F32)
                              xst = [S.tile(st, "xst%d" % i, [128, 1024], F32) for i in range(2)]
                              xn = [S.tile(st, "xn%d" % i, [128, 1024], BF16) for i in range(2)]
                              rmsnorm_T(lambda tt: mem[b, tt * 128:(tt + 1) * 128, :], MEM // 128, memT, 8, xst, xn)
                              wst6 = wst + [S.tile(st, "wstx%d" % i, [128, 512], F32) for i in range(4)]
                              load_w(wK, 1024, lambda c: w_kv[c * 128:(c + 1) * 128, :], stg=wst6)
                              op("pool", lambda e: e.memset(xva.h[:, :, :, 128:129], 1.0), w=[xva])
                              for h in range(4):
                                  ps = proj_fm(wK, h * 128, memT, 0, MEM)
                                  op("act", lambda e: e.activation(out=xkT.h[:, h, :], in_=ps.h[:, 0:MEM], func=AF.Copy), r=[ps], w=[xkT])
                              for mt in range(2):
                                  ps = proj_tm(wK, 512, 512, memT, mt * 128)
                                  op("act", lambda e: e.activation(out=xva.h[:, mt, :, 0:128], in_=ps.h[:, :].rearrange("p (h d) -> p h d", h=4),
                                                                   func=AF.Copy), r=[ps], w=[xva])
                              load_w(wA, 512, win_src(OFF["xq"], 512), stg=wst6)
                              load_w(wB, 512, win_src(OFF["xg"], 512), stg=wst6)
                              def x_proj(m):
                                  tok0 = 256 * m
                                  bi = m % 2
                                  for h in range(4):
                                      ps = proj_fm(wA, h * 128, hT, tok0, 256)
                                      op("act", lambda e: e.activation(out=xqT2[bi].h[:, h, :], in_=ps.h[:, 0:256], func=AF.Copy), r=[ps], w=[xqT2[bi]])
                                  for t in range(2):
                                      ps = proj_tm(wB, 0, 512, hT, tok0 + t * 128)
                                      op("act", lambda e: e.activation(out=sg2[bi][t].h[:, :], in_=ps.h[:, :], func=AF.Silu), r=[ps], w=[sg2[bi][t]])

                              def x_qk(m, h):
                                  bi = m % 2
                                  pss = pS[h % 2]
                                  for mt in range(2):
                                      op("pe", lambda e: e.matmul(pss.h[:, mt * 256:(mt + 1) * 256], lhsT=xkT.h[:, h, mt * 128:(mt + 1) * 128],
                                                                  rhs=xqT2[bi].h[:, h, :], start=True, stop=True), r=[xkT, xqT2[bi]], w=[pss])

                              def x_rest(m, h):
                                  bi = m % 2
                                  pss = pS[h % 2]
                                  ptt = PT[h % 2]
                                  acc = pV[h % 2]
                                  op("act", lambda e: e.activation(out=ptt.h[:, :, :], in_=pss.h[:, :].rearrange("p (a b) -> p a b", a=2),
                                                                   func=AF.Exp, scale=128.0 ** -0.5), r=[pss], w=[ptt])
                                  for mt in range(2):
                                      for t in range(2):
                                          op("pe", lambda e: e.matmul(acc.h[:, t * 129:(t + 1) * 129], lhsT=ptt.h[:, mt, t * 128:(t + 1) * 128],
                                                                      rhs=xva.h[:, mt, h, :], start=(mt == 0 and t == 0), stop=(mt == 1),
                                                                      skip_group_check=True), r=[ptt, xva], w=[acc])
                                  for t in range(2):
                                      op("dve", lambda e: e.reciprocal(rd.h[:, t:t + 1], acc.h[:, t * 129 + 128:t * 129 + 129]), r=[acc], w=[rd])
                                      op("dve", lambda e: e.tensor_scalar(out=yb2[bi][t].h[:, h * 128:(h + 1) * 128], in0=acc.h[:, t * 129:t * 129 + 128],
                                                                          scalar1=rd.h[:, t:t + 1], scalar2=None, op0=ALU.mult),
                                         r=[acc, rd], w=[yb2[bi][t]])

                              def x_tail(m):
                                  tok0 = 256 * m
                                  bi = m % 2
                                  for t in range(2):
                                      op("pool", lambda e: e.tensor_tensor(out=abt[t].h[:, :], in0=yb2[bi][t].h[:, :], in1=sg2[bi][t].h[:, :], op=ALU.mult),
                                         r=[yb2[bi][t], sg2[bi][t]], w=[abt[t]])
                                      for c in range(4):
                                          op("pe", lambda e: e.transpose(pT.h[:, c, :], abt[t].h[:, c * 128:(c + 1) * 128], identb.h[:, :]),
                                             r=[abt[t], identb], w=[pT])
                                      op("dve", lambda e: e.tensor_copy(aT_x.h[:, :, tok0 + t * 128:tok0 + (t + 1) * 128], pT.h[:, 0:4, :]),
                                         r=[pT], w=[aT_x])

                              x_proj(0)
                              for m in range(8):
                                  x_qk(m, 0)
                                  if m > 0:
                                      x_tail(m - 1)
                                  if m + 1 < 8:
                                      x_proj(m + 1)
                                  for h in range(4):
                                      if h + 1 < 4:
                                          x_qk(m, h + 1)
                                      x_rest(m, h)
                              x_tail(7)
                              S.barrier()
                          if b == 0:
                              dump("aT_x", aT_x, aT_x.h[:, :, :], (128, 4, SEQ))
                          S.barrier()
                          ck("cross")
                          with ExitStack() as st:
                              aTs = [aT_moba, aT_dsa, aT_x]
                              uT = S.tile(st, "uT", [128, 8, SEQ], BF16)
                              wO = S.tile(st, "wO", [128, 8, 1024], BF16)
                              gfb = S.tile(st, "gfb", [128, DM], F32)
                              stw = ExitStack()
                              w4 = [S.tile(stw, "w4st%d" % i, [128, 1024], F32) for i in range(2)]
                              wG = [S.tile(stw, "wG%d" % i, [128, 8, 384], BF16) for i in range(2)]
                              wU = [[S.tile(stw, "wU%d_%d" % (j, i), [128, 4, 128], BF16) for i in range(3)] for j in range(2)]
                              gsbs = [S.tile(stw, "gsb%d" % i, [128, 512], F32) for i in range(2)]
                              uaccs = [S.tile(stw, "uacc%d" % i, [128, 512], F32) for i in range(2)]
                              tmpus = [S.tile(stw, "tmpu%d" % i, [128, 512], F32) for i in range(2)]
                              pQ = [pA, pB, pS[0], pS[1]]
                              q_i = [0]
                              w4_i = [0]
                              S.dma("sp", gfb.h[:, :], gfin[:, :], gfb)

                              def load_fo(fo):
                                  bi = fo % 2
                                  for mi in range(3):
                                      stg = w4[w4_i[0] % 2]
                                      w4_i[0] += 1
                                      c0 = OFF["gl"] + mi * 1024 + fo * 128
                                      S.dma("sp", stg.h[:, :].rearrange("p (c n) -> p c n", c=8),
                                            w_in[:, c0:c0 + 128].rearrange("(c p) n -> p c n", p=128), stg)
                                      op("act", lambda e: e.activation(out=wG[bi].h[:, :, mi * 128:(mi + 1) * 128],
                                                                       in_=stg.h[:, :].rearrange("p (c n) -> p c n", c=8), func=AF.Copy),
                                         r=[stg], w=[wG[bi]])
                                      stg = w4[w4_i[0] % 2]
                                      w4_i[0] += 1
                                      S.dma("sp", stg.h[:, 0:512].rearrange("p (c n) -> p c n", c=4),
                                            w_up[mi, :, fo * 128:(fo + 1) * 128].rearrange("(c p) n -> p c n", p=128), stg)
                                      op("dve", lambda e: e.tensor_copy(wU[bi][mi].h[:, :, :],
                                                                        stg.h[:, 0:512].rearrange("p (c n) -> p c n", c=4)), r=[stg], w=[wU[bi][mi]])

                              load_fo(0)
                              for fo in range(8):
                                  bi = fo % 2
                                  if fo + 1 < 8:
                                      load_fo(fo + 1)
                                  else:
                                      for c in range(8):
                                          stg = w4[w4_i[0] % 2]
                                          w4_i[0] += 1
                                          S.dma("sp", stg.h[:, :], w_out[c * 128:(c + 1) * 128, :], stg)
                                          if c % 2 == 0:
                                              op("act", lambda e: e.activation(out=wO.h[:, c, :], in_=stg.h[:, :], func=AF.Copy), r=[stg], w=[wO])
                                          else:
                                              op("dve", lambda e: e.tensor_copy(wO.h[:, c, :], stg.h[:, :]), r=[stg], w=[wO])
                                  for tg in range(4):
                                      uacc = uaccs[tg % 2]
                                      for mi in range(3):
                                          gsb = gsbs[q_i[0] % 2]
                                          tmpu = tmpus[q_i[0] % 2]
                                          psg = proj_fm(wG[bi], mi * 128, hT, tg * 512, 512, ps=pQ[(2 * q_i[0]) % 4])
                                          op("act", lambda e: e.activation(out=gsb.h[:, :], in_=psg.h[:, :], func=AF.Sigmoid,
                                                                           bias=vec.h[:, 16 + mi * 8 + fo:17 + mi * 8 + fo]), r=[psg, vec], w=[gsb])
                                          psy = proj_fm(wU[bi][mi], 0, aTs[mi], tg * 512, 512, ps=pQ[(2 * q_i[0] + 1) % 4])
                                          q_i[0] += 1
                                          if mi == 0:
                                              op("dve", lambda e: e.tensor_tensor(out=uacc.h[:, :], in0=psy.h[:, :], in1=gsb.h[:, :], op=ALU.mult),
                                                 r=[psy, gsb], w=[uacc])
                                          else:
                                              op("dve", lambda e: e.tensor_tensor(out=tmpu.h[:, :], in0=psy.h[:, :], in1=gsb.h[:, :], op=ALU.mult),
                                                 r=[psy, gsb], w=[tmpu])
                                              if mi == 1:
                                                  op("pool", lambda e: e.tensor_tensor(out=uacc.h[:, :], in0=uacc.h[:, :], in1=tmpu.h[:, :], op=ALU.add),
                                                     r=[uacc, tmpu], w=[uacc])
                                              else:
                                                  op("pool", lambda e: e.tensor_tensor(out=uT.h[:, fo, tg * 512:(tg + 1) * 512], in0=uacc.h[:, :],
                                                                                       in1=tmpu.h[:, :], op=ALU.add), r=[uacc, tmpu], w=[uT])
                              S.barrier()
                              stw.close()
                              xst = [S.tile(st, "xst%d" % i, [128, 1024], F32) for i in range(4)]
                              xn = [S.tile(st, "xn%d" % i, [128, 1024], BF16) for i in range(2)]
                              for tt in range(SEQ // 128):
                                  xs = xst[xst_i[0] % len(xst)]
                                  xnn = xn[xst_i[0] % len(xn)]
                                  xst_i[0] += 1
                                  k = tt % 4
                                  ssv, rsv = ss_v[k], rs_v[k]
                                  S.dma("sp", xs.h[:, :], x[b, tt * 128:(tt + 1) * 128, :], xs)
                                  for half in range(2):
                                      ps = proj_tm(wO, half * 512, 512, uT, tt * 128)
                                      op("dve", lambda e: e.tensor_tensor(out=xs.h[:, half * 512:(half + 1) * 512], in0=ps.h[:, :],
                                                                          in1=xs.h[:, half * 512:(half + 1) * 512], op=ALU.add), r=[ps, xs], w=[xs])
                                  op("act", lambda e: e.activation(out=xnn.h[:, :], in_=xs.h[:, :], func=AF.Square,
                                                                   accum_out=smallf.h[:, k:k + 1]), r=[xs], w=[xnn, ssv])
                                  op("act", lambda e: e.activation(out=smallf.h[:, 4 + k:5 + k], in_=smallf.h[:, k:k + 1], func=AF.Sqrt,
                                                                   scale=1.0 / DM, bias=epst.h[:, 0:1]), r=[ssv, epst], w=[rsv])
                                  op("dve", lambda e: e.reciprocal(smallf.h[:, 4 + k:5 + k], smallf.h[:, 4 + k:5 + k]), r=[rsv], w=[rsv])
                                  op("dve", lambda e: e.scalar_tensor_tensor(out=xs.h[:, :], in0=xs.h[:, :], scalar=smallf.h[:, 4 + k:5 + k],
                                                                             in1=gfb.h[:, :], op0=ALU.mult, op1=ALU.mult),
                                     r=[xs, rsv, gfb], w=[xs])
                                  S.dma("pool", y[b, tt * 128:(tt + 1) * 128, :], xs.h[:, :], xs, load=False)
                              S.barrier()
        except StopBuild:
            pass
        S.dead = False
        S.finish()
    return nc


def rope_tables():
    half = 32
    inv = np.power(np.float32(10000.0), -np.arange(half, dtype=np.float32) * np.float32(2.0) / np.float32(64)).astype(np.float32)
    pos = np.arange(SEQ, dtype=np.float32)
    ang = (pos[:, None] * inv[None, :]).astype(np.float32)
    cos = np.cos(ang).astype(np.float32).T
    sin = np.sin(ang).astype(np.float32).T
    cosT = np.tile(cos, (4, 1))
    sinT = np.tile(sin, (4, 1))
    return np.ascontiguousarray(np.concatenate([cosT, sinT], axis=1))


def const_tables():
    c = np.zeros((128, C_END), np.float32)
    c[:, C_ID:C_ID + 128] = np.eye(128, dtype=np.float32)
    R = np.zeros((128, 128), np.float32)
    for hb in (0, 64):
        for d in range(32):
            R[hb + d + 32, hb + d] = -1.0
            R[hb + d, hb + d + 32] = 1.0
    c[:, C_R:C_R + 128] = R
    k = np.arange(128)[:, None]
    q = np.arange(256)[None, :]
    c[:, C_CBT:C_CBT + 256] = np.where(k <= q, 0.0, NEG)
    c[:, C_CBT + 256:C_CBT + 512] = np.where(k + 128 <= q, 0.0, NEG)
    qq = np.arange(128)[:, None]
    kk = np.arange(256)[None, :]
    c[:, C_CBQ:C_CBQ + 256] = np.where(kk <= qq, 0.0, NEGF)
    c[:, C_CBQ + 256:C_CBQ + 512] = np.where(kk <= qq + 128, 0.0, NEGF)
    c[:, C_PW:C_PW + NITER] = (0.5 ** np.arange(1, NITER + 1, dtype=np.float64)).astype(np.float32)[None, :]
    return c


def make_in_maps(x, mem, g_in, w_in, b_merge, g_mem, w_mem_kv, w_up_moba, w_up_dsa, w_up_cross, w_out, g_final):
    f = lambda a: np.ascontiguousarray(np.asarray(a, dtype=np.float32))
    x, mem = f(x), f(mem)
    vecs = np.zeros((128, 40), np.float32)
    vecs[:, 0:8] = f(g_in)[0].reshape(8, 128).T
    vecs[:, 8:16] = f(g_mem)[0].reshape(8, 128).T
    vecs[:, 16:40] = f(b_merge)[0].reshape(24, 128).T
    gfin = np.ascontiguousarray(np.broadcast_to(f(g_final)[None, :], (128, DM)))
    wup = np.ascontiguousarray(np.stack([f(w_up_moba)[0], f(w_up_dsa)[0], f(w_up_cross)[0]], axis=0))
    shared = dict(w_in=f(w_in)[0], w_kv=f(w_mem_kv)[0], w_up=wup, w_out=f(w_out)[0], vecs=vecs, gfin=gfin,
                  ropet=rope_tables(), consts=const_tables())
    maps = []
    for c in range(NCORES):
        d = dict(shared)
        d["x"] = np.ascontiguousarray(x[c * NB:(c + 1) * NB])
        d["mem"] = np.ascontiguousarray(mem[c * NB:(c + 1) * NB])
        maps.append(d)
    return maps


def kernel(**inputs):
    nc = build()
    maps = make_in_maps(**inputs)
    res = run_bass_kernel_spmd(nc, maps, core_ids=list(range(NCORES)))
    out = np.concatenate([np.asarray(r["y"]) for r in res.results], axis=0)
    return out.astype(np.float32)
```
